# Optimizing a Trainium2 kernel written in Bass

```python
import math
import jax, jax.numpy as jnp
from jax import lax
import numpy as np

D_MODEL = 1024
BATCH = 1
SEQ = 16384
DEPTH = 1

HEAD_DIM = 64
HEADS_PER_GROUP = D_MODEL // 128
ATT_GROUPS = ((128, 1), (512, 4), (2048, 16))
N_GROUPS = 3
ATT_QKV_WIDTH = N_GROUPS * HEADS_PER_GROUP * HEAD_DIM
ATT_OUT_WIDTH = HEADS_PER_GROUP * HEAD_DIM
CONV_WIDTH = D_MODEL
CONV_K = 3
DN_ALPHA = (2.0 * DEPTH) ** 0.25
DN_BETA = (8.0 * DEPTH) ** -0.25
LN_EPS = 1e-5
SECTION_WIDTHS = (ATT_QKV_WIDTH, ATT_QKV_WIDTH, ATT_QKV_WIDTH, ATT_OUT_WIDTH,
                  CONV_WIDTH, CONV_WIDTH, CONV_WIDTH, CONV_WIDTH, 2 * D_MODEL)
IN_WIDTH = 4 * ATT_QKV_WIDTH // 4 * 3 + ATT_OUT_WIDTH + 4 * CONV_WIDTH + 2 * D_MODEL
V_START = 2 * ATT_QKV_WIDTH
V_END = 3 * ATT_QKV_WIDTH

kernel_name = "hybrid_conv_dilated_swa_deepnorm"


def layer_norm(x, g, b):
    xf = x.astype(jnp.float32)
    mu = jnp.mean(xf, axis=-1, keepdims=True)
    var = jnp.mean(jnp.square(xf - mu), axis=-1, keepdims=True)
    y = (xf - mu) * lax.rsqrt(var + LN_EPS) * g.astype(jnp.float32) + b.astype(jnp.float32)
    return y.astype(x.dtype)


def causal_short_conv(u, w):
    s = u.shape[1]
    up = jnp.pad(u, ((0, 0), (CONV_K - 1, 0), (0, 0)))
    y = w[0] * up[:, 0:s]
    for j in range(1, CONV_K):
        y = y + w[j] * up[:, j:j + s]
    return y


def dilated_window_attention(q, k, v, window, dilation):
    b, s, h, dh = q.shape
    n_win = window // dilation
    blk = n_win
    length = s // dilation
    n_blk = -(-length // blk)
    pad = n_blk * blk - length

    def to_blocks(t):
        t = t.reshape(b, length, dilation, h, dh).transpose(0, 2, 1, 3, 4)
        t = jnp.pad(t, ((0, 0), (0, 0), (0, pad), (0, 0), (0, 0)))
        return t.reshape(b, dilation, n_blk, blk, h, dh)

    def with_prev(t):
        prev = jnp.pad(t, ((0, 0), (0, 0), (1, 0), (0, 0), (0, 0), (0, 0)))[:, :, :-1]
        return jnp.concatenate([prev, t], axis=3)

    qb = to_blocks(q)
    kc = with_prev(to_blocks(k))
    vc = with_prev(to_blocks(v))

    scores = jnp.einsum('brnqhd,brnkhd->brnhqk', qb, kc,
                        preferred_element_type=jnp.float32) * (dh ** -0.5)
    q_idx = jnp.arange(blk)[:, None]
    k_idx = jnp.arange(2 * blk)[None, :] - blk
    rel = q_idx - k_idx
    blk_start = (jnp.arange(n_blk) * blk)[:, None, None]
    valid = (rel >= 0) & (rel <= n_win) & (blk_start + k_idx >= 0)
    scores = jnp.where(valid[:, None], scores, -jnp.inf)
    lse = jax.nn.logsumexp(scores, axis=-1)
    probs = jnp.exp(scores - lse[..., None])
    out = jnp.einsum('brnhqk,brnkhd->brnqhd', probs, vc.astype(jnp.float32))

    def from_blocks(t):
        rest = t.shape[4:]
        t = t.reshape((b, dilation, n_blk * blk) + rest)[:, :, :length]
        return jnp.swapaxes(t, 1, 2).reshape((b, s) + rest)

    return from_blocks(out), from_blocks(lse.transpose(0, 1, 2, 4, 3))


def hybrid_layer(x, w_in, conv_w, w_conv_out, w_att_out, b_gate, w_o, ln_g, ln_b):
    b, s, _ = x.shape
    proj = jnp.einsum('bsd,de->bse', x, w_in)
    idx = np.cumsum(SECTION_WIDTHS)[:-1]
    q, k, v, g_att, h_c, b_c, c_c, g_conv, gate_logits = jnp.split(proj, idx, axis=-1)

    q = q.reshape(b, s, N_GROUPS, HEADS_PER_GROUP, HEAD_DIM)
    k = k.reshape(b, s, N_GROUPS, HEADS_PER_GROUP, HEAD_DIM)
    v = v.reshape(b, s, N_GROUPS, HEADS_PER_GROUP, HEAD_DIM)
    outs, lses = [], []
    for gi, (window, dilation) in enumerate(ATT_GROUPS):
        o, l = dilated_window_attention(q[:, :, gi], k[:, :, gi], v[:, :, gi], window, dilation)
        outs.append(o)
        lses.append(l)
    outs = jnp.stack(outs)
    mix = jax.nn.softmax(jnp.stack(lses), axis=0)
    att = jnp.sum(mix[..., None] * outs, axis=0).reshape(b, s, ATT_OUT_WIDTH).astype(x.dtype)
    y_att = jnp.einsum('bse,ed->bsd', att * jax.nn.silu(g_att), w_att_out)

    conv = causal_short_conv(c_c * h_c, conv_w)
    y_conv = jnp.einsum('bse,ed->bsd', (b_c * conv) * jax.nn.silu(g_conv), w_conv_out)

    gates = jax.nn.sigmoid(gate_logits + b_gate)
    g_c, g_a = jnp.split(gates, 2, axis=-1)
    merged = g_c * y_conv + g_a * y_att
    out = jnp.einsum('bsd,de->bse', merged, w_o)

    return layer_norm(DN_ALPHA * x + out, ln_g, ln_b)


def setup_inputs(seed: int = 0) -> dict:
    key = jax.random.key(seed)
    ks = jax.random.split(key, 10)
    x = jax.random.normal(ks[0], (BATCH, SEQ, D_MODEL), jnp.float32)
    w_in = jax.random.normal(ks[1], (DEPTH, D_MODEL, IN_WIDTH), jnp.float32) * D_MODEL ** -0.5
    w_in = w_in.at[:, :, V_START:V_END].multiply(DN_BETA)
    conv_w = jax.random.normal(ks[2], (DEPTH, CONV_K, CONV_WIDTH), jnp.float32) * CONV_K ** -0.5
    w_conv_out = jax.random.normal(ks[3], (DEPTH, CONV_WIDTH, D_MODEL), jnp.float32) * (CONV_WIDTH ** -0.5 * DN_BETA)
    w_att_out = jax.random.normal(ks[4], (DEPTH, ATT_OUT_WIDTH, D_MODEL), jnp.float32) * (ATT_OUT_WIDTH ** -0.5 * DN_BETA)
    b_gate = jax.random.normal(ks[5], (DEPTH, 2 * D_MODEL), jnp.float32) * 0.1
    w_o = jax.random.normal(ks[6], (DEPTH, D_MODEL, D_MODEL), jnp.float32) * (D_MODEL ** -0.5 * DN_BETA)
    ln_g = 1.0 + 0.02 * jax.random.normal(ks[7], (DEPTH, D_MODEL), jnp.float32)
    ln_b = 0.02 * jax.random.normal(ks[8], (DEPTH, D_MODEL), jnp.float32)
    return {"x": x, "w_in": w_in, "conv_w": conv_w, "w_conv_out": w_conv_out,
            "w_att_out": w_att_out, "b_gate": b_gate, "w_o": w_o, "ln_g": ln_g, "ln_b": ln_b}


def reference(x, w_in, conv_w, w_conv_out, w_att_out, b_gate, w_o, ln_g, ln_b):
    h = x
    for layer in range(DEPTH):
        h = hybrid_layer(h, w_in[layer], conv_w[layer], w_conv_out[layer], w_att_out[layer],
                         b_gate[layer], w_o[layer], ln_g[layer], ln_b[layer])
    return h
```

```python
import numpy as np
import concourse.bass as bass
import concourse.mybir as mybir
from concourse.bass_utils import run_bass_kernel_spmd

F32 = mybir.dt.float32
BF16 = mybir.dt.bfloat16
AF = mybir.ActivationFunctionType
ALU = mybir.AluOpType

NCORES = 8
SEQ = 16384
DM = 1024
TOK = SEQ // NCORES
XC = 2 * TOK
DILS = (1, 4, 16)
GORD = (1, 2, 0)
ALPHA = 2.0 ** 0.25
LN_EPS = 1e-5
NUNITS = 30
UW = 4096

Q0, K0, V0, GA0, H0, B0, C0, GC0, GL0 = 0, 1536, 3072, 4608, 5120, 6144, 7168, 8192, 9216


class Sched:
    ENGS = ("pe", "act", "dve", "pool", "sp")

    def __init__(self, nc, esems, dma_sems):
        self.nc = nc
        self.semobj = {"e:" + e: s for e, s in esems.items()}
        self.free_dma = list(dma_sems)
        self.ops = {e: [] for e in self.ENGS}
        self.count = {e: 0 for e in self.ENGS}
        self.lastw = {}
        self.readers = {}
        self.bank_last = {}
        self.waited = {e: {} for e in self.ENGS}
        self.dma_cnt = {}
        self.n_ops = 0

    def _slot(self, slot):
        key = "d:" + slot
        if key not in self.semobj:
            self.semobj[key] = self.free_dma.pop()
            self.dma_cnt[key] = 0
        return key

    def add(self, eng, fn, reads=(), writes=(), banks=(), dma=None, ndma=1):
        is_dma = dma is not None
        deps = {}

        def need(tok, kind):
            if tok is None:
                return
            semkey, val, teng, tdma = tok
            if not tdma and teng == eng and not is_dma and eng == "pe":
                return
            if deps.get(semkey, 0) < val:
                deps[semkey] = val

        for r in reads:
            need(self.lastw.get(r), "raw")
        for w in writes:
            need(self.lastw.get(w), "waw")
            for t in self.readers.get(w, ()):
                need(t, "war")
        for b in banks:
            t = self.bank_last.get(b)
            if t is not None and t[2] != eng:
                need(t, "bank")

        if is_dma:
            key = self._slot(dma)
            self.dma_cnt[key] += 16 * ndma
            tok = (key, self.dma_cnt[key], eng, True)
        else:
            self.count[eng] += 1
            tok = ("e:" + eng, self.count[eng], eng, False)
        for r in reads:
            self.readers.setdefault(r, []).append(tok)
        for w in writes:
            self.lastw[w] = tok
            self.readers[w] = []
        for b in banks:
            self.bank_last[b] = tok
        self.ops[eng].append((fn, deps, tok))
        self.n_ops += 1
        return tok

    def emit(self, final_waits=()):
        nc = self.nc
        with nc.Block() as block:
            deco = {"pe": block.tensor, "act": block.scalar, "dve": block.vector,
                    "pool": block.gpsimd, "sp": block.sync}
            for eng in self.ENGS:
                oplist = self.ops[eng]
                fw = list(final_waits) if eng == "sp" else []
                if not oplist and not fw:
                    continue

                def body(e, oplist=oplist, eng=eng, fw=fw):
                    waited = self.waited[eng]
                    for fn, deps, tok in oplist:
                        for semkey, val in deps.items():
                            if waited.get(semkey, 0) >= val:
                                continue
                            e.wait_ge(self.semobj[semkey], val)
                            waited[semkey] = val
                        res = fn(e)
                        if tok[3]:
                            for inst in res:
                                inst.then_inc(self.semobj[tok[0]], 16)
                        else:
                            res.then_inc(self.semobj[tok[0]], 1)
                    for semkey, val, _, _ in fw:
                        if waited.get(semkey, 0) >= val:
                            continue
                        e.wait_ge(self.semobj[semkey], val)
                        waited[semkey] = val

                deco[eng](body)
        self.ops = {e: [] for e in self.ENGS}


def _unit(wc):
    k, n = wc.shape
    nk = k // 128
    return np.ascontiguousarray(wc.reshape(nk, 128, n).transpose(1, 0, 2)).reshape(128, nk * n)


def _prep_weights(w_in, conv_w, w_conv_out, w_att_out, b_gate, w_o, ln_g, ln_b):
    W = np.asarray(w_in[0], np.float32)
    wco = np.asarray(w_conv_out[0], np.float32)
    wao = np.asarray(w_att_out[0], np.float32)
    ws = np.zeros((NUNITS, 128, UW), np.float32)
    zeros128 = np.zeros((DM, 128), np.float32)
    for hp in range(4):
        for i, g in enumerate(GORD):
            c = g * 512 + hp * 128
            cols = [W[:, Q0 + c:Q0 + c + 128], W[:, K0 + c:K0 + c + 128], W[:, V0 + c:V0 + c + 128],
                    W[:, GA0 + hp * 128:GA0 + hp * 128 + 128] if i == 1 else zeros128]
            ws[hp * 3 + i] = _unit(np.concatenate(cols, axis=1))
    for cc in range(8):
        c = cc * 128
        cols = [W[:, H0 + c:H0 + c + 128], W[:, B0 + c:B0 + c + 128],
                W[:, C0 + c:C0 + c + 128], W[:, GC0 + c:GC0 + c + 128]]
        ws[12 + cc] = _unit(np.concatenate(cols, axis=1))
    for j in range(8):
        c = j * 128
        gw = np.concatenate([W[:, GL0 + c:GL0 + c + 128], W[:, GL0 + 1024 + c:GL0 + 1024 + c + 128]], axis=1)
        ws[20 + j, :, 0:2048] = _unit(gw)
        ws[20 + j, :, 2048:3072] = _unit(wco[:, c:c + 128])
        ws[20 + j, :, 3072:3584] = _unit(wao[:, c:c + 128])
    wof = np.asarray(w_o[0], np.float32)
    ws[28] = _unit(wof[:, 0:512])
    ws[29] = _unit(wof[:, 512:1024])
    cst = np.zeros((128, 40), np.float32)
    cw = np.asarray(conv_w[0], np.float32)
    cst[:, 0:24] = cw.reshape(3, 8, 128).transpose(2, 1, 0).reshape(128, 24)
    cst[:, 24:40] = np.asarray(b_gate[0], np.float32).reshape(16, 128).T
    gb = np.concatenate([np.broadcast_to(np.asarray(ln_g[0], np.float32), (128, DM)),
                         np.broadcast_to(np.asarray(ln_b[0], np.float32), (128, DM))], axis=1)
    return ws, cst, np.ascontiguousarray(gb)


def _masks(core):
    j = np.arange(128)[:, None]
    i = np.arange(128)[None, :]
    p = (j >= i).astype(np.float32)
    c = (j <= i).astype(np.float32)
    ph = p * (1.0 if core > 0 else 0.0)
    ident = np.eye(128, dtype=np.float32)
    return np.ascontiguousarray(np.concatenate([ph, c, p, c, p, c, p, c, ph, c, ph, c, ident], axis=1))


MKW = 1536 + 128


def build(debug=False):
    from contextlib import ExitStack
    nc = bass.Bass("TRN2", target_bir_lowering=False)
    xT_d = nc.dram_tensor("xT", [8, 128, XC], F32, kind="ExternalInput").ap()
    xtok_d = nc.dram_tensor("xtok", [TOK, DM], F32, kind="ExternalInput").ap()
    ws_d = nc.dram_tensor("ws", [NUNITS, 128, UW], F32, kind="ExternalInput").ap()
    cst_d = nc.dram_tensor("cst", [128, 40], F32, kind="ExternalInput").ap()
    gb_d = nc.dram_tensor("gb", [128, 2 * DM], F32, kind="ExternalInput").ap()
    mk_d = nc.dram_tensor("mk", [128, MKW], F32, kind="ExternalInput").ap()
    out_d = nc.dram_tensor("out", [TOK, DM], F32, kind="ExternalOutput").ap()
    if debug:
        dbg_a = nc.dram_tensor("dbg_a", [128, 4 * TOK], BF16, kind="ExternalOutput").ap()
        dbg_u = nc.dram_tensor("dbg_u", [128, 8 * TOK], BF16, kind="ExternalOutput").ap()
        dbg_m = nc.dram_tensor("dbg_m", [128, 8 * TOK], BF16, kind="ExternalOutput").ap()

    with ExitStack() as es:
        def sb(name, shape, dt):
            return es.enter_context(nc.sbuf_tensor(name, shape, dt))

        mk = sb("mk_bf", [128, MKW], BF16)
        ident = mk[:, 1536:1664]
        cst = sb("cst_sb", [128, 40], F32)
        wr = [sb(f"wr{i}", [128, UW], BF16) for i in range(3)]
        low = sb("low32", [128, 16384], BF16)
        mT = low[:, :].rearrange("p (k t) -> p k t", k=8)
        es123 = ExitStack()
        xT = es123.enter_context(nc.sbuf_tensor("xT_bf", [128, 8, XC], BF16))
        aT = es123.enter_context(nc.sbuf_tensor("aT", [128, 4, TOK], BF16))
        pp = [es.enter_context(nc.psum_tensor(f"pp{i}", [128, 1024], F32)) for i in range(4)]
        ps = [pp[i // 2][:, (i % 2) * 512:(i % 2 + 1) * 512] for i in range(8)]
        esems = {e: es.enter_context(nc.semaphore("sem_" + e)) for e in Sched.ENGS}
        dsems = [es.enter_context(nc.semaphore(f"dsem{i}")) for i in range(32)]
        S = Sched(nc, esems, dsems)

        def load_unit(u):
            if u >= NUNITS:
                return
            buf = wr[u % 3]
            if u < 12 and u % 3 != 1:
                o_ap = buf[:, :].rearrange("p (k c) -> p k c", c=512)[:, :, 0:384]
                i_ap = ws_d[u].rearrange("p (k c) -> p k c", c=512)[:, :, 0:384]
            else:
                o_ap, i_ap = buf[:], ws_d[u]
            S.add("pool", lambda e: [e.dma_start(out=o_ap, in_=i_ap)],
                  writes=[("wr", u % 3)], dma=f"wr{u % 3}")

        def wv(u, width):
            return wr[u % 3][:, 0:8 * width].rearrange("p (k c) -> p k c", c=width)

        load_unit(0)
        xsrc = xT_d.rearrange("k p c -> p k c")
        xpieces = [(3, 1536, 2048), (4, 2048, 2560), (5, 2560, 3072), (6, 3072, 3584), (7, 3584, 4096),
                   (2, 1024, 1536), (1, 512, 1024), (0, 0, 512)]
        for (nm, c0, c1) in xpieces:
            S.add("pool", lambda e, c0=c0, c1=c1: [e.dma_start(out=xT[:, :, c0:c1], in_=xsrc[:, :, c0:c1])],
                  writes=[("xT", nm)], dma=f"xT{nm}")
            if nm == 4:
                S.add("pool", lambda e: [e.dma_start(out=mk[:], in_=mk_d[:, :])], writes=["mk"], dma="mk")
            if nm == 7:
                load_unit(1)
        S.add("sp", lambda e: [e.dma_start(out=cst[:], in_=cst_d[:, :])], writes=["cst"], dma="cst")

        def xres(c0, n):
            out = []
            for (nm, a, b) in xpieces:
                if a < c0 + n and c0 < b:
                    out.append(("xT", nm))
            return out

        def proj(bank, lhs_fn, rhs_fn, n, nk=8, reads=()):
            lhs = [lhs_fn(kc) for kc in range(nk)]
            rhs = [rhs_fn(kc) for kc in range(nk)]

            def fn(e):
                inst = None
                for kc in range(nk):
                    inst = e.matmul(ps[bank][:, 0:n], lhs[kc], rhs[kc],
                                    start=(kc == 0), stop=(kc == nk - 1))
                return inst
            S.add("pe", fn, reads=list(reads), banks=[bank])

        with ExitStack() as p1:
            def sb1(name, shape, dt):
                return p1.enter_context(nc.sbuf_tensor(name, shape, dt))
            Vp1t = sb1("Vp1", [128, 8192], BF16)
            Vp = [low[:, 0:8192].rearrange("p (b h c) -> p b h c", h=2, c=128),
                  Vp1t[:, :].rearrange("p (b h c) -> p b h c", h=2, c=128)]
            PT = [[low[:, 8192 + (2 * h + i) * 512:8192 + (2 * h + i + 1) * 512] for i in range(2)] for h in range(2)]
            QT = [low[:, 10240:12288], sb1("QT1", [128, TOK], BF16)]
            KT = [low[:, 12288:16384], sb1("KT1", [128, 2 * TOK], BF16)]
            VT = sb1("VT0", [128, XC], BF16)
            Uacc = sb1("Uacc", [128, 2, TOK], F32)
            gatt = sb1("gatt", [128, TOK], F32)
            rz = sb1("rz", [128, 2, 512], F32)

            S.add("dve", lambda e: e.memset(low[:, 0:8192], 1.0),
                  writes=[("Vp", 0, gi, h) for gi in range(4) for h in range(2)])
            S.add("dve", lambda e: e.memset(Vp1t[:, :], 1.0),
                  writes=[("Vp", 1, gi, h) for gi in range(4) for h in range(2)])

            pj_rot = [0]

            def next_pj():
                b = pj_rot[0]
                pj_rot[0] = (b + 1) % 2
                return b

            SB = {0: [2, 3], 1: [4, 5]}
            UB = {0: 6, 1: 7}

            def ktiles(d):
                tiles = []
                t0 = -128 * d
                while t0 < 0:
                    n = min(512, -t0)
                    tiles.append((t0, n))
                    t0 += n
                return tiles + [(512 * w, 512) for w in range(4)]

            def proj_jobs(u):
                hp, ui = divmod(u, 3)
                g = GORD[ui]
                d = DILS[g]
                nb = 16 // d
                buf = u % 2
                W = wv(u, 512)
                wres = ("wr", u % 3)
                jobs = []
                tiles = ktiles(d)
                klen = (nb + 1) * 128

                def vt_job(ti, t0, n):
                    b = next_pj()
                    proj(b, lambda kc: W[:, kc, 256:384], lambda kc: xT[:, kc, TOK + t0:TOK + t0 + n], n,
                         reads=[wres] + xres(TOK + t0, n))
                    S.add("act", lambda e: e.activation(out=VT[:, TOK + t0:TOK + t0 + n], in_=ps[b][:, 0:n], func=AF.Copy),
                          writes=[("VT", ti)], banks=[b])

                def k_job(ti, t0, n):
                    b = next_pj()
                    proj(b, lambda kc: W[:, kc, 128:256], lambda kc: xT[:, kc, TOK + t0:TOK + t0 + n], n,
                         reads=[wres] + xres(TOK + t0, n))
                    lk0 = (t0 + 128 * d) // d
                    cnt = n // d
                    if d == 1:
                        o_ap = KT[buf][:, lk0:lk0 + cnt]
                        i_ap = ps[b][:, 0:n]
                    else:
                        o_ap = KT[buf][:, 0:d * klen].rearrange("p (r l) -> p r l", r=d)[:, :, lk0:lk0 + cnt]
                        i_ap = ps[b][:, 0:n].rearrange("p (l r) -> p r l", r=d)
                    S.add("act", lambda e: e.activation(out=o_ap, in_=i_ap, func=AF.Copy),
                          writes=[("KT", buf, ti)], banks=[b])

                def q_job(w):
                    b = next_pj()
                    proj(b, lambda kc: W[:, kc, 0:128], lambda kc: xT[:, kc, TOK + 512 * w:TOK + 512 * (w + 1)], 512,
                         reads=[wres, ("xT", 4 + w)])
                    lk0 = 512 * w // d
                    cnt = 512 // d
                    if d == 1:
                        o_ap = QT[buf][:, lk0:lk0 + cnt]
                        i_ap = ps[b][:, 0:512]
                    else:
                        o_ap = QT[buf][:, :].rearrange("p (r l) -> p r l", r=d)[:, :, lk0:lk0 + cnt]
                        i_ap = ps[b][:, 0:512].rearrange("p (l r) -> p r l", r=d)
                    S.add("act", lambda e: e.activation(out=o_ap, in_=i_ap, func=AF.Copy, scale=0.125),
                          writes=[("QT", buf, w)], banks=[b])

                def g_job(w):
                    b = next_pj()
                    proj(b, lambda kc: W[:, kc, 384:512], lambda kc: xT[:, kc, TOK + 512 * w:TOK + 512 * (w + 1)], 512,
                         reads=[wres, ("xT", 4 + w)])
                    S.add("act", lambda e: e.activation(out=gatt[:, 512 * w:512 * (w + 1)], in_=ps[b][:, 0:512], func=AF.Silu),
                          writes=[("gatt", w)], banks=[b])

                def t_job(b0):
                    nblk = d * (nb + 1)
                    nbk = min(8, nblk - b0)
                    b = next_pj()
                    pbf = ps[b][:, :].bitcast(BF16)

                    def fnt(e):
                        inst = None
                        for q in range(nbk):
                            vb = b0 + q
                            r, kb = vb // (nb + 1), vb % (nb + 1)
                            st = TOK - 128 * d + kb * 128 * d + r
                            inst = e.transpose(pbf[:, q * 128:(q + 1) * 128], VT[:, st:st + 127 * d + 1:d], ident)
                        return inst
                    S.add("pe", fnt, reads=["mk"] + [("VT", ti) for ti in range(len(tiles))], banks=[b])
                    for h in range(2):
                        o_ap = Vp[buf][:, b0:b0 + nbk, h, 64 * h:64 * h + 64]
                        i_ap = pbf[:, 0:nbk * 128].rearrange("p (q c) -> p q c", c=128)[:, :, 64 * h:64 * h + 64]
                        S.add("dve", lambda e, o=o_ap, i=i_ap: e.tensor_copy(out=o, in_=i),
                              writes=[("Vp", buf, b0 // 8, h)], banks=[b])

                tjobs = [(lambda b0=b0: t_job(b0)) for b0 in range(0, d * (nb + 1), 8)]
                if u == 0:
                    for ti, (t0, n) in enumerate(tiles):
                        jobs.append(lambda ti=ti, t0=t0, n=n: vt_job(ti, t0, n))
                        jobs.append(lambda ti=ti, t0=t0, n=n: k_job(ti, t0, n))
                        if t0 >= 0:
                            jobs.append(lambda w=t0 // 512: q_job(w))
                    jobs += tjobs
                elif u == 1:
                    own = [(ti, t0, n) for ti, (t0, n) in enumerate(tiles) if t0 >= 0]
                    halo = [(ti, t0, n) for ti, (t0, n) in enumerate(tiles) if t0 < 0]
                    for (ti, t0, n) in own:
                        jobs.append(lambda ti=ti, t0=t0, n=n: vt_job(ti, t0, n))
                    for (ti, t0, n) in own:
                        jobs.append(lambda ti=ti, t0=t0, n=n: k_job(ti, t0, n))
                    for w in range(4):
                        jobs.append(lambda w=w: q_job(w))
                    for w in range(4):
                        jobs.append(lambda w=w: g_job(w))
                    for (ti, t0, n) in reversed(halo):
                        jobs.append(lambda ti=ti, t0=t0, n=n: vt_job(ti, t0, n))
                    for (ti, t0, n) in reversed(halo):
                        jobs.append(lambda ti=ti, t0=t0, n=n: k_job(ti, t0, n))
                    jobs += tjobs
                    return jobs
                else:
                    for ti, (t0, n) in enumerate(tiles):
                        jobs.append(lambda ti=ti, t0=t0, n=n: vt_job(ti, t0, n))
                    jobs += tjobs
                    for ti, (t0, n) in enumerate(tiles):
                        jobs.append(lambda ti=ti, t0=t0, n=n: k_job(ti, t0, n))
                    for w in range(4):
                        jobs.append(lambda w=w: q_job(w))
                if ui == 1:
                    for w in range(4):
                        jobs.append(lambda w=w: g_job(w))
                return jobs

            def step(u):
                hp, ui = divmod(u, 3)
                g = GORD[ui]
                d = DILS[g]
                nb = 16 // d
                buf = u % 2
                ntile = len(ktiles(d))
                load_unit(u + 2)
                qres = [("QT", buf, w) for w in range(4)]
                kres = [("KT", buf, ti) for ti in range(ntile)]

                def sbank_desc(j):
                    slots = []
                    for q in (2 * j, 2 * j + 1):
                        r, qb = q // nb, q % nb
                        kprev = r * (nb + 1) + qb
                        slots.append((kprev, q))
                        slots.append((kprev + 1, q))
                    return slots

                def qk(j, h):
                    slots = sbank_desc(j)
                    bank = SB[h][j % 2]
                    groups = []
                    s = 0
                    while s < 4:
                        if s + 1 < 4 and slots[s + 1][0] == slots[s][0] and slots[s + 1][1] == slots[s][1] + 1:
                            groups.append((s, 2))
                            s += 2
                        else:
                            groups.append((s, 1))
                            s += 1

                    def fn(e):
                        inst = None
                        for (s0, cnt) in groups:
                            kblk, q = slots[s0]
                            inst = e.matmul(ps[bank][:, s0 * 128:(s0 + cnt) * 128],
                                            KT[buf][64 * h:64 * h + 64, kblk * 128:(kblk + 1) * 128],
                                            QT[buf][64 * h:64 * h + 64, q * 128:(q + cnt) * 128],
                                            start=True, stop=True)
                        return inst
                    S.add("pe", fn, reads=qres + kres, banks=[bank])

                def softmax_part(j, h):
                    bank = SB[h][j % 2]
                    pt = PT[h][j % 2]
                    if nb == 1:
                        mvar = 2
                    else:
                        mvar = 0 if (2 * j) % nb == 0 else 1
                    S.add("act", lambda e: e.activation(out=pt, in_=ps[bank][:, :], func=AF.Exp),
                          writes=[("PT", h, j % 2)], banks=[bank])
                    S.add("dve", lambda e: e.tensor_tensor(out=pt, in0=pt, in1=mk[:, mvar * 512:(mvar + 1) * 512],
                                                           op=ALU.mult),
                          reads=["mk", ("PT", h, j % 2)], writes=[("PT", h, j % 2)])

                def pv(j, h):
                    slots = sbank_desc(j)
                    pt = PT[h][j % 2]
                    ub = UB[h]

                    def fn(e):
                        inst = None
                        for s in range(4):
                            kblk, q = slots[s]
                            c0 = (q % 4) * 128
                            inst = e.matmul(ps[ub][:, c0:c0 + 128], Vp[buf][:, kblk, h, :], pt[:, s * 128:(s + 1) * 128],
                                            start=(s % 2 == 0), stop=(s % 2 == 1))
                        return inst
                    vres = sorted({("Vp", buf, slots[s][0] // 8, h) for s in range(4)})
                    S.add("pe", fn, reads=vres + [("PT", h, j % 2)], banks=[ub])

                def uevac(m, h):
                    ub = UB[h]
                    L = TOK // d
                    if d == 1:
                        o_ap = Uacc[:, h, 512 * m:512 * (m + 1)]
                        i_ap = ps[ub][:, :]
                    elif L >= 512:
                        r = (512 * m) // L
                        l0 = (512 * m) % L
                        o_ap = Uacc[:, h, :].rearrange("p (l r) -> p r l", r=d)[:, r, l0:l0 + 512]
                        i_ap = ps[ub][:, :]
                    else:
                        r0 = (512 * m) // L
                        nr = 512 // L
                        o_ap = Uacc[:, h, :].rearrange("p (l r) -> p r l", r=d)[:, r0:r0 + nr, :]
                        i_ap = ps[ub][:, :].rearrange("p (r l) -> p r l", r=nr)
                    allu = [("Uacc", h, mm) for mm in range(4)]
                    if ui == 0:
                        S.add("dve", lambda e: e.tensor_copy(out=o_ap, in_=i_ap),
                              writes=allu, banks=[ub])
                    else:
                        S.add("dve", lambda e: e.tensor_tensor(out=o_ap, in0=o_ap, in1=i_ap, op=ALU.add),
                              reads=allu, writes=[("Uacc", h, m)], banks=[ub])

                def finalize_piece(m):
                    sl = slice(512 * m, 512 * (m + 1))
                    rzm = rz[:, m % 2, :]
                    zs = ((Uacc[64:128, 0, sl], rzm[0:64, :], 0), (Uacc[0:64, 1, sl], rzm[64:128, :], 1))
                    for (z, ro, h) in zs:
                        S.add("act", lambda e, z=z: e.activation(out=z, in_=z, func=AF.Ln),
                              reads=[("Uacc", h, m)], writes=[("Uacc", h, m)])
                        S.add("act", lambda e, z=z, ro=ro: e.activation(out=ro, in_=z, func=AF.Exp, scale=-1.0),
                              reads=[("Uacc", h, m)], writes=[("rz", h, m % 2)])
                    S.add("dve", lambda e: e.tensor_tensor(out=rzm, in0=rzm, in1=gatt[:, sl], op=ALU.mult),
                          reads=[("rz", 0, m % 2), ("rz", 1, m % 2), ("gatt", m)], writes=[("rz", 0, m % 2), ("rz", 1, m % 2)])
                    S.add("dve", lambda e: e.tensor_tensor(out=aT[0:64, hp, sl], in0=Uacc[0:64, 0, sl], in1=rzm[0:64, :], op=ALU.mult),
                          reads=[("Uacc", 0, m), ("rz", 0, m % 2)], writes=[("aT", hp, 0, m)])
                    S.add("dve", lambda e: e.tensor_tensor(out=aT[64:128, hp, sl], in0=Uacc[64:128, 1, sl], in1=rzm[64:128, :], op=ALU.mult),
                          reads=[("Uacc", 1, m), ("rz", 1, m % 2)], writes=[("aT", hp, 1, m)])

                jobs = proj_jobs(u + 1) if u + 1 < 12 else []
                cuts = [(len(jobs) * j) // 8 for j in range(9)]
                pending = list(deferred)
                del deferred[:]
                for h in range(2):
                    qk(0, h)
                for j in range(8):
                    for h in range(2):
                        softmax_part(j, h)
                    if j + 1 < 8:
                        for h in range(2):
                            qk(j + 1, h)
                    for job in jobs[cuts[j]:cuts[j + 1]]:
                        job()
                    if j == 0 and pending:
                        for fz in pending:
                            fz()
                    if ui == 2 and j >= 2 and j % 2 == 0:
                        finalize_piece(j // 2 - 1)
                    for h in range(2):
                        pv(j, h)
                    if j % 2 == 1:
                        for h in range(2):
                            uevac(j // 2, h)
                        if ui == 2 and j == 7:
                            deferred.append(lambda: finalize_piece(3))

            deferred = []
            for job in proj_jobs(0):
                job()
            for u in range(12):
                step(u)
            for fz in deferred:
                fz()
            if debug:
                S.add("sp", lambda e: [e.dma_start(out=dbg_a[:, :], in_=aT[:].rearrange("p a t -> p (a t)"))],
                      reads=[("aT", hp, h, m) for hp in range(4) for h in range(2) for m in range(4)], dma="dbg")
            S.emit()

        es23 = ExitStack()
        uT = es23.enter_context(nc.sbuf_tensor("uT", [128, 8, TOK], BF16))
        bank_rot = [0]

        def next_bank():
            b = bank_rot[0]
            bank_rot[0] = (b + 1) % 8
            return b

        with ExitStack() as p2:
            def sb2(name, shape, dt):
                return p2.enter_context(nc.sbuf_tensor(name, shape, dt))
            hS = sb2("hS", [128, TOK + 2], F32)
            chS = sb2("chS", [128, TOK + 2], F32)
            acc = sb2("acc", [128, TOK], F32)
            sg = sb2("sg", [128, TOK], F32)
            ttiles = [(TOK - 2, 2, 0)] + [(TOK + 512 * w, 512, 2 + 512 * w) for w in range(4)]
            allch = [("chS", d0) for (_, _, d0) in ttiles]
            allacc = [("acc", w) for w in range(4)]
            for cc in range(8):
                u = 12 + cc
                load_unit(u + 2)
                W = wv(u, 512)
                wres = ("wr", u % 3)
                for (c0, n, d0) in ttiles:
                    b = next_bank()
                    proj(b, lambda kc: W[:, kc, 0:128], lambda kc, c0=c0, n=n: xT[:, kc, c0:c0 + n], n,
                         reads=[wres] + xres(c0, n))
                    S.add("act", lambda e, b=b, n=n, d0=d0: e.activation(out=hS[:, d0:d0 + n], in_=ps[b][:, 0:n], func=AF.Copy),
                          writes=[("hS", d0)], banks=[b])
                for (c0, n, d0) in ttiles:
                    b = next_bank()
                    proj(b, lambda kc: W[:, kc, 256:384], lambda kc, c0=c0, n=n: xT[:, kc, c0:c0 + n], n,
                         reads=[wres] + xres(c0, n))
                    S.add("dve", lambda e, b=b, n=n, d0=d0: e.tensor_tensor(out=chS[:, d0:d0 + n], in0=ps[b][:, 0:n],
                                                                            in1=hS[:, d0:d0 + n], op=ALU.mult),
                          reads=[("hS", d0)], writes=[("chS", d0)], banks=[b])
                w0 = cst[:, cc * 3 + 0:cc * 3 + 1]
                w1 = cst[:, cc * 3 + 1:cc * 3 + 2]
                w2 = cst[:, cc * 3 + 2:cc * 3 + 3]
                S.add("act", lambda e, w2=w2: e.activation(out=acc[:, :], in_=chS[:, 2:TOK + 2], func=AF.Copy, scale=w2),
                      reads=allch + ["cst"], writes=allacc)
                S.add("dve", lambda e, w1=w1: e.scalar_tensor_tensor(out=acc[:, :], in0=chS[:, 1:TOK + 1], scalar=w1,
                                                                     in1=acc[:, :], op0=ALU.mult, op1=ALU.add),
                      reads=allch + allacc + ["cst"], writes=allacc)
                S.add("dve", lambda e, w0=w0: e.scalar_tensor_tensor(out=acc[:, :], in0=chS[:, 0:TOK], scalar=w0,
                                                                     in1=acc[:, :], op0=ALU.mult, op1=ALU.add),
                      reads=allch + allacc + ["cst"], writes=allacc)
                for w in range(4):
                    b = next_bank()
                    proj(b, lambda kc: W[:, kc, 384:512], lambda kc, w=w: xT[:, kc, TOK + 512 * w:TOK + 512 * (w + 1)], 512,
                         reads=[wres, ("xT", 4 + w)])
                    S.add("act", lambda e, b=b, w=w: e.activation(out=sg[:, 512 * w:512 * (w + 1)], in_=ps[b][:, :], func=AF.Silu),
                          writes=[("sg", w)], banks=[b])
                for w in range(4):
                    b = next_bank()
                    sl = slice(512 * w, 512 * (w + 1))
                    proj(b, lambda kc: W[:, kc, 128:256], lambda kc, w=w: xT[:, kc, TOK + 512 * w:TOK + 512 * (w + 1)], 512,
                         reads=[wres, ("xT", 4 + w)])
                    S.add("dve", lambda e, b=b, sl=sl: e.tensor_tensor(out=acc[:, sl], in0=ps[b][:, :], in1=acc[:, sl], op=ALU.mult),
                          reads=[("acc", w)], writes=[("acc", w)], banks=[b])
                    S.add("dve", lambda e, sl=sl, cc=cc: e.tensor_tensor(out=uT[:, cc, sl], in0=acc[:, sl], in1=sg[:, sl], op=ALU.mult),
                          reads=[("acc", w), ("sg", w)], writes=[("uT", cc, w)])
            if debug:
                S.add("sp", lambda e: [e.dma_start(out=dbg_u[:, :], in_=uT[:].rearrange("p a t -> p (a t)"))],
                      reads=[("uT", cc, w) for cc in range(8) for w in range(4)], dma="dbg")
            S.emit()

        with ExitStack() as p3:
            def sb3(name, shape, dt):
                return p3.enter_context(nc.sbuf_tensor(name, shape, dt))
            gcS = [sb3(f"gcS{i}", [128, 512], F32) for i in range(2)]
            gaS = [sb3(f"gaS{i}", [128, 512], F32) for i in range(2)]
            mS = [sb3(f"mS{i}", [128, 512], F32) for i in range(2)]
            it = 0
            for j in range(8):
                u = 20 + j
                load_unit(u + 2)
                Wg = wr[u % 3][:, 0:2048].rearrange("p (k c) -> p k c", c=256)
                Wco = wr[u % 3][:, 2048:3072].rearrange("p (k c) -> p k c", c=128)
                Wao = wr[u % 3][:, 3072:3584].rearrange("p (k c) -> p k c", c=128)
                wres = ("wr", u % 3)
                for w in range(4):
                    i2 = it % 2
                    it += 1
                    tsl = slice(512 * w, 512 * (w + 1))
                    xsl = slice(TOK + 512 * w, TOK + 512 * (w + 1))
                    b = next_bank()
                    proj(b, lambda kc: Wg[:, kc, 0:128], lambda kc, xsl=xsl: xT[:, kc, xsl], 512, reads=[wres, ("xT", 4 + w)])
                    S.add("act", lambda e, b=b, i2=i2, j=j: e.activation(out=gcS[i2][:, :], in_=ps[b][:, :], func=AF.Sigmoid,
                                                                         bias=cst[:, 24 + j:25 + j]),
                          reads=["cst"], writes=[("gcS", i2)], banks=[b])
                    b = next_bank()
                    proj(b, lambda kc: Wg[:, kc, 128:256], lambda kc, xsl=xsl: xT[:, kc, xsl], 512, reads=[wres, ("xT", 4 + w)])
                    S.add("act", lambda e, b=b, i2=i2, j=j: e.activation(out=gaS[i2][:, :], in_=ps[b][:, :], func=AF.Sigmoid,
                                                                         bias=cst[:, 32 + j:33 + j]),
                          reads=["cst"], writes=[("gaS", i2)], banks=[b])
                    b = next_bank()
                    proj(b, lambda kc: Wco[:, kc, :], lambda kc, tsl=tsl: uT[:, kc, tsl], 512,
                         reads=[wres] + [("uT", cc, w) for cc in range(8)])
                    S.add("dve", lambda e, b=b, i2=i2: e.tensor_tensor(out=mS[i2][:, :], in0=ps[b][:, :], in1=gcS[i2][:, :], op=ALU.mult),
                          reads=[("gcS", i2)], writes=[("mS", i2)], banks=[b])
                    b = next_bank()
                    proj(b, lambda kc: Wao[:, kc, :], lambda kc, tsl=tsl: aT[:, kc, tsl], 512, nk=4,
                         reads=[wres] + [("aT", hp, h, w) for hp in range(4) for h in range(2)])
                    S.add("dve", lambda e, b=b, i2=i2: e.tensor_tensor(out=gaS[i2][:, :], in0=ps[b][:, :], in1=gaS[i2][:, :], op=ALU.mult),
                          reads=[("gaS", i2)], writes=[("gaS", i2)], banks=[b])
                    S.add("dve", lambda e, i2=i2, j=j, tsl=tsl: e.tensor_tensor(out=mT[:, j, tsl], in0=mS[i2][:, :], in1=gaS[i2][:, :], op=ALU.add),
                          reads=[("mS", i2), ("gaS", i2)], writes=[("mT", j, w)])
            if debug:
                S.add("sp", lambda e: [e.dma_start(out=dbg_m[:, :], in_=low[:, :])],
                      reads=[("mT", j, w) for j in range(8) for w in range(4)], dma="dbg")
            S.emit()

        es23.close()
        es123.close()
        with ExitStack() as p4:
            def sb4(name, shape, dt):
                return p4.enter_context(nc.sbuf_tensor(name, shape, dt))
            gb = sb4("gb_sb", [128, 2 * DM], F32)
            NY = 4
            xt = [sb4(f"xt{i}", [128, DM], F32) for i in range(3)]
            ys = [sb4(f"ys{i}", [128, DM], F32) for i in range(NY)]
            ob = [sb4(f"ob{i}", [128, DM], F32) for i in range(3)]
            sa = [sb4(f"sa{i}", [128, 64], F32) for i in range(6)]
            junk = sb4("junk", [128, DM], BF16)
            sd = [sb4(f"sd{i}", [128, 64], F32) for i in range(6)]
            wo = [wv(28 + hf, 512) for hf in range(2)]
            S.add("sp", lambda e: [e.dma_start(out=gb[:], in_=gb_d[:, :])], writes=["gb"], dma="gb")
            S.add("dve", lambda e: e.tensor_copy(out=pp[3][:, :], in_=gb[:, 0:DM]),
                  reads=["gb"], writes=["gbp"], banks=[6, 7])
            out_toks = []

            def load_xt(t):
                S.add("sp", lambda e: [e.dma_start(out=xt[t % 3][:], in_=xtok_d[128 * t:128 * (t + 1), :])],
                      writes=[("xt", t % 3)], dma=f"xt{t % 3}")

            load_xt(0)
            load_xt(1)

            def stage_a1(t):
                i3, iy, i6, pb = t % 3, t % NY, t % 6, t % 3
                if t + 2 < 16:
                    load_xt(t + 2)
                for hf in range(2):
                    proj(2 * pb + hf, lambda kc: mT[:, kc, 128 * t:128 * (t + 1)], lambda kc, hf=hf: wo[hf][:, kc, :], 512,
                         reads=[("mT", j, t // 4) for j in range(8)] + [("wr", (28 + hf) % 3)])
                S.add("dve", lambda e: e.scalar_tensor_tensor(
                    out=ys[iy][:, :], in0=xt[i3][:, :], scalar=ALPHA, in1=pp[pb][:, :], op0=ALU.mult, op1=ALU.add),
                    reads=[("xt", i3)], writes=[("ys", iy)], banks=[2 * pb, 2 * pb + 1])
                S.add("act", lambda e: e.activation(out=junk[:, :], in_=ys[iy][:, :], func=AF.Copy, scale=1.0 / DM,
                                                    accum_out=sa[i6][:, 0:1]),
                      reads=[("ys", iy)], writes=["junk", ("sa", i6, 0)])
                S.add("act", lambda e: e.activation(out=junk[:, :], in_=ys[iy][:, :], func=AF.Square, scale=DM ** -0.5,
                                                    accum_out=sa[i6][:, 1:2]),
                      reads=[("ys", iy)], writes=["junk", ("sa", i6, 1)])

            def stage_a2(t):
                i6 = t % 6
                S.add("dve", lambda e: e.tensor_tensor(out=sd[i6][:, 0:1], in0=sa[i6][:, 0:1], in1=sa[i6][:, 0:1], op=ALU.mult),
                      reads=[("sa", i6, 0)], writes=[("sd", i6, 0)])
                S.add("dve", lambda e: e.tensor_scalar(out=sd[i6][:, 4:5], in0=sa[i6][:, 0:1], scalar1=-1.0, scalar2=None, op0=ALU.mult),
                      reads=[("sa", i6, 0)], writes=[("sd", i6, 4)])
                S.add("dve", lambda e: e.tensor_tensor(out=sd[i6][:, 1:2], in0=sa[i6][:, 1:2], in1=sd[i6][:, 0:1], op=ALU.subtract),
                      reads=[("sa", i6, 1), ("sd", i6, 0)], writes=[("sd", i6, 1)])
                S.add("act", lambda e: e.activation(out=sa[i6][:, 2:3], in_=sd[i6][:, 1:2], func=AF.Sqrt, bias=LN_EPS),
                      reads=[("sd", i6, 1)], writes=[("sa", i6, 2)])

            def stage_b(t):
                i3, iy, i6 = t % 3, t % NY, t % 6
                S.add("dve", lambda e: e.reciprocal(out=sd[i6][:, 2:3], in_=sa[i6][:, 2:3]),
                      reads=[("sa", i6, 2)], writes=[("sd", i6, 2)])
                S.add("dve", lambda e: e.tensor_tensor(out=sd[i6][:, 3:4], in0=sd[i6][:, 4:5], in1=sd[i6][:, 2:3], op=ALU.mult),
                      reads=[("sd", i6, 4), ("sd", i6, 2)], writes=[("sd", i6, 3)])
                S.add("act", lambda e: e.activation(out=ob[i3][:, :], in_=ys[iy][:, :], func=AF.Identity,
                                                    scale=sd[i6][:, 2:3], bias=sd[i6][:, 3:4]),
                      reads=[("ys", iy), ("sd", i6, 2), ("sd", i6, 3)], writes=[("ob", i3)])

            def stage_d(t):
                i3 = t % 3
                S.add("dve", lambda e: e.tensor_tensor(out=ob[i3][:, :], in0=ob[i3][:, :], in1=pp[3][:, :], op=ALU.mult),
                      reads=[("ob", i3), "gbp"], writes=[("ob", i3)], banks=[6, 7])
                S.add("pool", lambda e: e.tensor_tensor(out=ob[i3][:, :], in0=ob[i3][:, :], in1=gb[:, DM:2 * DM], op=ALU.add),
                      reads=[("ob", i3), "gb"], writes=[("ob", i3)])
                tok = S.add("pool", lambda e: [e.dma_start(out=out_d[128 * t:128 * (t + 1), :], in_=ob[i3][:])],
                            reads=[("ob", i3)], dma=f"ob{i3}")
                out_toks.append(tok)

            for t in range(16 + 3):
                if t < 16:
                    stage_a1(t)
                if 0 <= t - 2 < 16:
                    stage_b(t - 2)
                if 0 <= t - 1 < 16:
                    stage_a2(t - 1)
                if 0 <= t - 3 < 16:
                    stage_d(t - 3)
            finals = out_toks[-3:]
            if debug:
                finals = finals + [("d:dbg", S.dma_cnt["d:dbg"], "sp", True)]
            S.emit(final_waits=finals)
    return nc


_NC_CACHE = {}


def _get_nc(debug=False):
    if debug not in _NC_CACHE:
        _NC_CACHE[debug] = build(debug)
    return _NC_CACHE[debug]


def make_in_maps(x, w_in, conv_w, w_conv_out, w_att_out, b_gate, w_o, ln_g, ln_b):
    x2 = np.asarray(x, np.float32).reshape(SEQ, DM)
    ws, cst, gb = _prep_weights(w_in, conv_w, w_conv_out, w_att_out, b_gate, w_o, ln_g, ln_b)
    in_maps = []
    for c in range(NCORES):
        own = x2[c * TOK:(c + 1) * TOK]
        halo = x2[(c - 1) * TOK:c * TOK] if c > 0 else np.zeros((TOK, DM), np.float32)
        xe = np.concatenate([halo, own], axis=0)
        xT = np.ascontiguousarray(xe.T).reshape(8, 128, XC)
        in_maps.append({"xT": xT, "xtok": np.ascontiguousarray(own), "ws": ws,
                        "cst": cst, "gb": gb, "mk": _masks(c)})
    return in_maps


def kernel(x, w_in, conv_w, w_conv_out, w_att_out, b_gate, w_o, ln_g, ln_b):
    nc = _get_nc(False)
    in_maps = make_in_maps(x, w_in, conv_w, w_conv_out, w_att_out, b_gate, w_o, ln_g, ln_b)
    res = run_bass_kernel_spmd(nc, in_maps, core_ids=list(range(NCORES)))
    out = np.concatenate([r["out"] for r in res.results], axis=0)
    return out.reshape(1, SEQ, DM).astype(np.float32)
```

```python
import numpy as np
import concourse.bass as bass
import concourse.mybir as mybir
from concourse.bass_utils import run_bass_kernel_spmd

F32 = mybir.dt.float32
BF16 = mybir.dt.bfloat16
AF = mybir.ActivationFunctionType
ALU = mybir.AluOpType

NCORES = 8
SEQ = 16384
DM = 1024
TOK = SEQ // NCORES
XC = 2 * TOK
DILS = (1, 4, 16)
GORD = (1, 2, 0)
ALPHA = 2.0 ** 0.25
LN_EPS = 1e-5
NUNITS = 30
UW = 4096

Q0, K0, V0, GA0, H0, B0, C0, GC0, GL0 = 0, 1536, 3072, 4608, 5120, 6144, 7168, 8192, 9216


class Sched:
    ENGS = ("pe", "act", "dve", "pool", "sp")

    def __init__(self, nc, esems, dma_sems):
        self.nc = nc
        self.semobj = {"e:" + e: s for e, s in esems.items()}
        self.free_dma = list(dma_sems)
        self.ops = {e: [] for e in self.ENGS}
        self.count = {e: 0 for e in self.ENGS}
        self.lastw = {}
        self.readers = {}
        self.bank_last = {}
        self.waited = {e: {} for e in self.ENGS}
        self.dma_cnt = {}
        self.n_ops = 0

    def _slot(self, slot):
        key = "d:" + slot
        if key not in self.semobj:
            self.semobj[key] = self.free_dma.pop()
            self.dma_cnt[key] = 0
        return key

    def add(self, eng, fn, reads=(), writes=(), banks=(), dma=None, ndma=1):
        is_dma = dma is not None
        deps = {}

        def need(tok, kind):
            if tok is None:
                return
            semkey, val, teng, tdma = tok
            if not tdma and teng == eng and not is_dma and eng == "pe":
                return
            if deps.get(semkey, 0) < val:
                deps[semkey] = val

        for r in reads:
            need(self.lastw.get(r), "raw")
        for w in writes:
            need(self.lastw.get(w), "waw")
            for t in self.readers.get(w, ()):
                need(t, "war")
        for b in banks:
            t = self.bank_last.get(b)
            if t is not None and t[2] != eng:
                need(t, "bank")

        if is_dma:
            key = self._slot(dma)
            self.dma_cnt[key] += 16 * ndma
            tok = (key, self.dma_cnt[key], eng, True)
        else:
            self.count[eng] += 1
            tok = ("e:" + eng, self.count[eng], eng, False)
        for r in reads:
            self.readers.setdefault(r, []).append(tok)
        for w in writes:
            self.lastw[w] = tok
            self.readers[w] = []
        for b in banks:
            self.bank_last[b] = tok
        self.ops[eng].append((fn, deps, tok))
        self.n_ops += 1
        return tok

    def emit(self, final_waits=()):
        nc = self.nc
        with nc.Block() as block:
            deco = {"pe": block.tensor, "act": block.scalar, "dve": block.vector,
                    "pool": block.gpsimd, "sp": block.sync}
            for eng in self.ENGS:
                oplist = self.ops[eng]
                fw = list(final_waits) if eng == "sp" else []
                if not oplist and not fw:
                    continue

                def body(e, oplist=oplist, eng=eng, fw=fw):
                    waited = self.waited[eng]
                    for fn, deps, tok in oplist:
                        for semkey, val in deps.items():
                            if waited.get(semkey, 0) >= val:
                                continue
                            e.wait_ge(self.semobj[semkey], val)
                            waited[semkey] = val
                        res = fn(e)
                        if tok[3]:
                            for inst in res:
                                inst.then_inc(self.semobj[tok[0]], 16)
                        else:
                            res.then_inc(self.semobj[tok[0]], 1)
                    for semkey, val, _, _ in fw:
                        if waited.get(semkey, 0) >= val:
                            continue
                        e.wait_ge(self.semobj[semkey], val)
                        waited[semkey] = val

                deco[eng](body)
        self.ops = {e: [] for e in self.ENGS}


def _unit(wc):
    k, n = wc.shape
    nk = k // 128
    return np.ascontiguousarray(wc.reshape(nk, 128, n).transpose(1, 0, 2)).reshape(128, nk * n)


def _prep_weights(w_in, conv_w, w_conv_out, w_att_out, b_gate, w_o, ln_g, ln_b):
    W = np.asarray(w_in[0], np.float32)
    wco = np.asarray(w_conv_out[0], np.float32)
    wao = np.asarray(w_att_out[0], np.float32)
    ws = np.zeros((NUNITS, 128, UW), np.float32)
    zeros128 = np.zeros((DM, 128), np.float32)
    for hp in range(4):
        for i, g in enumerate(GORD):
            c = g * 512 + hp * 128
            cols = [W[:, Q0 + c:Q0 + c + 128], W[:, K0 + c:K0 + c + 128], W[:, V0 + c:V0 + c + 128],
                    W[:, GA0 + hp * 128:GA0 + hp * 128 + 128] if i == 1 else zeros128]
            ws[hp * 3 + i] = _unit(np.concatenate(cols, axis=1))
    for cc in range(8):
        c = cc * 128
        cols = [W[:, H0 + c:H0 + c + 128], W[:, B0 + c:B0 + c + 128],
                W[:, C0 + c:C0 + c + 128], W[:, GC0 + c:GC0 + c + 128]]
        ws[12 + cc] = _unit(np.concatenate(cols, axis=1))
    for j in range(8):
        c = j * 128
        gw = np.concatenate([W[:, GL0 + c:GL0 + c + 128], W[:, GL0 + 1024 + c:GL0 + 1024 + c + 128]], axis=1)
        ws[20 + j, :, 0:2048] = _unit(gw)
        ws[20 + j, :, 2048:3072] = _unit(wco[:, c:c + 128])
        ws[20 + j, :, 3072:3584] = _unit(wao[:, c:c + 128])
    wof = np.asarray(w_o[0], np.float32)
    ws[28] = _unit(wof[:, 0:512])
    ws[29] = _unit(wof[:, 512:1024])
    cst = np.zeros((128, 40), np.float32)
    cw = np.asarray(conv_w[0], np.float32)
    cst[:, 0:24] = cw.reshape(3, 8, 128).transpose(2, 1, 0).reshape(128, 24)
    cst[:, 24:40] = np.asarray(b_gate[0], np.float32).reshape(16, 128).T
    gb = np.concatenate([np.broadcast_to(np.asarray(ln_g[0], np.float32), (128, DM)),
                         np.broadcast_to(np.asarray(ln_b[0], np.float32), (128, DM))], axis=1)
    return ws, cst, np.ascontiguousarray(gb)


def _masks(core):
    j = np.arange(128)[:, None]
    i = np.arange(128)[None, :]
    p = (j >= i).astype(np.float32)
    c = (j <= i).astype(np.float32)
    ph = p * (1.0 if core > 0 else 0.0)
    ident = np.eye(128, dtype=np.float32)
    return np.ascontiguousarray(np.concatenate([ph, c, p, c, p, c, p, c, ph, c, ph, c, ident], axis=1))


MKW = 1536 + 128


def build(debug=False):
    from contextlib import ExitStack
    nc = bass.Bass("TRN2", target_bir_lowering=False)
    xT_d = nc.dram_tensor("xT", [8, 128, XC], F32, kind="ExternalInput").ap()
    xtok_d = nc.dram_tensor("xtok", [TOK, DM], F32, kind="ExternalInput").ap()
    ws_d = nc.dram_tensor("ws", [NUNITS, 128, UW], F32, kind="ExternalInput").ap()
    cst_d = nc.dram_tensor("cst", [128, 40], F32, kind="ExternalInput").ap()
    gb_d = nc.dram_tensor("gb", [128, 2 * DM], F32, kind="ExternalInput").ap()
    mk_d = nc.dram_tensor("mk", [128, MKW], F32, kind="ExternalInput").ap()
    out_d = nc.dram_tensor("out", [TOK, DM], F32, kind="ExternalOutput").ap()
    if debug:
        dbg_a = nc.dram_tensor("dbg_a", [128, 4 * TOK], BF16, kind="ExternalOutput").ap()
        dbg_u = nc.dram_tensor("dbg_u", [128, 8 * TOK], BF16, kind="ExternalOutput").ap()
        dbg_m = nc.dram_tensor("dbg_m", [128, 8 * TOK], BF16, kind="ExternalOutput").ap()

    with ExitStack() as es:
        def sb(name, shape, dt):
            return es.enter_context(nc.sbuf_tensor(name, shape, dt))

        mk = sb("mk_bf", [128, MKW], BF16)
        ident = mk[:, 1536:1664]
        cst = sb("cst_sb", [128, 40], F32)
        wr = [sb(f"wr{i}", [128, UW], BF16) for i in range(3)]
        low = sb("low32", [128, 16384], BF16)
        mT = low[:, :].rearrange("p (k t) -> p k t", k=8)
        es123 = ExitStack()
        xT = es123.enter_context(nc.sbuf_tensor("xT_bf", [128, 8, XC], BF16))
        aT = es123.enter_context(nc.sbuf_tensor("aT", [128, 4, TOK], BF16))
        pp = [es.enter_context(nc.psum_tensor(f"pp{i}", [128, 1024], F32)) for i in range(4)]
        ps = [pp[i // 2][:, (i % 2) * 512:(i % 2 + 1) * 512] for i in range(8)]
        esems = {e: es.enter_context(nc.semaphore("sem_" + e)) for e in Sched.ENGS}
        dsems = [es.enter_context(nc.semaphore(f"dsem{i}")) for i in range(32)]
        S = Sched(nc, esems, dsems)

        def load_unit(u):
            if u >= NUNITS:
                return
            buf = wr[u % 3]
            if u < 12 and u % 3 != 1:
                o_ap = buf[:, :].rearrange("p (k c) -> p k c", c=512)[:, :, 0:384]
                i_ap = ws_d[u].rearrange("p (k c) -> p k c", c=512)[:, :, 0:384]
            else:
                o_ap, i_ap = buf[:], ws_d[u]
            S.add("pool", lambda e: [e.dma_start(out=o_ap, in_=i_ap)],
                  writes=[("wr", u % 3)], dma=f"wr{u % 3}")

        def wv(u, width):
            return wr[u % 3][:, 0:8 * width].rearrange("p (k c) -> p k c", c=width)

        load_unit(0)
        xsrc = xT_d.rearrange("k p c -> p k c")
        xpieces = [(3, 1536, 2048), (4, 2048, 2560), (5, 2560, 3072), (6, 3072, 3584), (7, 3584, 4096),
                   (2, 1024, 1536), (1, 512, 1024), (0, 0, 512)]
        for (nm, c0, c1) in xpieces:
            S.add("pool", lambda e, c0=c0, c1=c1: [e.dma_start(out=xT[:, :, c0:c1], in_=xsrc[:, :, c0:c1])],
                  writes=[("xT", nm)], dma=f"xT{nm}")
            if nm == 4:
                S.add("pool", lambda e: [e.dma_start(out=mk[:], in_=mk_d[:, :])], writes=["mk"], dma="mk")
            if nm == 7:
                load_unit(1)
        S.add("sp", lambda e: [e.dma_start(out=cst[:], in_=cst_d[:, :])], writes=["cst"], dma="cst")

        def xres(c0, n):
            out = []
            for (nm, a, b) in xpieces:
                if a < c0 + n and c0 < b:
                    out.append(("xT", nm))
            return out

        def proj(bank, lhs_fn, rhs_fn, n, nk=8, reads=()):
            lhs = [lhs_fn(kc) for kc in range(nk)]
            rhs = [rhs_fn(kc) for kc in range(nk)]

            def fn(e):
                inst = None
                for kc in range(nk):
                    inst = e.matmul(ps[bank][:, 0:n], lhs[kc], rhs[kc],
                                    start=(kc == 0), stop=(kc == nk - 1))
                return inst
            S.add("pe", fn, reads=list(reads), banks=[bank])

        with ExitStack() as p1:
            def sb1(name, shape, dt):
                return p1.enter_context(nc.sbuf_tensor(name, shape, dt))
            Vp1t = sb1("Vp1", [128, 8192], BF16)
            Vp = [low[:, 0:8192].rearrange("p (b h c) -> p b h c", h=2, c=128),
                  Vp1t[:, :].rearrange("p (b h c) -> p b h c", h=2, c=128)]
            PT = [[low[:, 8192 + (2 * h + i) * 512:8192 + (2 * h + i + 1) * 512] for i in range(2)] for h in range(2)]
            QT = [low[:, 10240:12288], sb1("QT1", [128, TOK], BF16)]
            KT = [low[:, 12288:16384], sb1("KT1", [128, 2 * TOK], BF16)]
            VT = sb1("VT0", [128, XC], BF16)
            Uacc = sb1("Uacc", [128, 2, TOK], F32)
            gatt = sb1("gatt", [128, TOK], F32)
            rz = sb1("rz", [128, 2, 512], F32)

            S.add("dve", lambda e: e.memset(low[:, 0:8192], 1.0),
                  writes=[("Vp", 0, gi, h) for gi in range(4) for h in range(2)])
            S.add("dve", lambda e: e.memset(Vp1t[:, :], 1.0),
                  writes=[("Vp", 1, gi, h) for gi in range(4) for h in range(2)])

            pj_rot = [0]

            def next_pj():
                b = pj_rot[0]
                pj_rot[0] = (b + 1) % 2
                return b

            SB = {0: [2, 3], 1: [4, 5]}
            UB = {0: 6, 1: 7}

            def ktiles(d):
                tiles = []
                t0 = -128 * d
                while t0 < 0:
                    n = min(512, -t0)
                    tiles.append((t0, n))
                    t0 += n
                return tiles + [(512 * w, 512) for w in range(4)]

            def proj_jobs(u):
                hp, ui = divmod(u, 3)
                g = GORD[ui]
                d = DILS[g]
                nb = 16 // d
                buf = u % 2
                W = wv(u, 512)
                wres = ("wr", u % 3)
                jobs = []
                tiles = ktiles(d)
                klen = (nb + 1) * 128

                def vt_job(ti, t0, n):
                    b = next_pj()
                    proj(b, lambda kc: W[:, kc, 256:384], lambda kc: xT[:, kc, TOK + t0:TOK + t0 + n], n,
                         reads=[wres] + xres(TOK + t0, n))
                    S.add("act", lambda e: e.activation(out=VT[:, TOK + t0:TOK + t0 + n], in_=ps[b][:, 0:n], func=AF.Copy),
                          writes=[("VT", ti)], banks=[b])

                def k_job(ti, t0, n):
                    b = next_pj()
                    proj(b, lambda kc: W[:, kc, 128:256], lambda kc: xT[:, kc, TOK + t0:TOK + t0 + n], n,
                         reads=[wres] + xres(TOK + t0, n))
                    lk0 = (t0 + 128 * d) // d
                    cnt = n // d
                    if d == 1:
                        o_ap = KT[buf][:, lk0:lk0 + cnt]
                        i_ap = ps[b][:, 0:n]
                    else:
                        o_ap = KT[buf][:, 0:d * klen].rearrange("p (r l) -> p r l", r=d)[:, :, lk0:lk0 + cnt]
                        i_ap = ps[b][:, 0:n].rearrange("p (l r) -> p r l", r=d)
                    S.add("act", lambda e: e.activation(out=o_ap, in_=i_ap, func=AF.Copy),
                          writes=[("KT", buf, ti)], banks=[b])

                def q_job(w):
                    b = next_pj()
                    proj(b, lambda kc: W[:, kc, 0:128], lambda kc: xT[:, kc, TOK + 512 * w:TOK + 512 * (w + 1)], 512,
                         reads=[wres, ("xT", 4 + w)])
                    lk0 = 512 * w // d
                    cnt = 512 // d
                    if d == 1:
                        o_ap = QT[buf][:, lk0:lk0 + cnt]
                        i_ap = ps[b][:, 0:512]
                    else:
                        o_ap = QT[buf][:, :].rearrange("p (r l) -> p r l", r=d)[:, :, lk0:lk0 + cnt]
                        i_ap = ps[b][:, 0:512].rearrange("p (l r) -> p r l", r=d)
                    S.add("act", lambda e: e.activation(out=o_ap, in_=i_ap, func=AF.Copy, scale=0.125),
                          writes=[("QT", buf, w)], banks=[b])

                def g_job(w):
                    b = next_pj()
                    proj(b, lambda kc: W[:, kc, 384:512], lambda kc: xT[:, kc, TOK + 512 * w:TOK + 512 * (w + 1)], 512,
                         reads=[wres, ("xT", 4 + w)])
                    S.add("act", lambda e: e.activation(out=gatt[:, 512 * w:512 * (w + 1)], in_=ps[b][:, 0:512], func=AF.Silu),
                          writes=[("gatt", w)], banks=[b])

                def t_job(b0):
                    nblk = d * (nb + 1)
                    nbk = min(8, nblk - b0)
                    b = next_pj()
                    pbf = ps[b][:, :].bitcast(BF16)

                    def fnt(e):
                        inst = None
                        for q in range(nbk):
                            vb = b0 + q
                            r, kb = vb // (nb + 1), vb % (nb + 1)
                            st = TOK - 128 * d + kb * 128 * d + r
                            inst = e.transpose(pbf[:, q * 128:(q + 1) * 128], VT[:, st:st + 127 * d + 1:d], ident)
                        return inst
                    S.add("pe", fnt, reads=["mk"] + [("VT", ti) for ti in range(len(tiles))], banks=[b])
                    for h in range(2):
                        o_ap = Vp[buf][:, b0:b0 + nbk, h, 64 * h:64 * h + 64]
                        i_ap = pbf[:, 0:nbk * 128].rearrange("p (q c) -> p q c", c=128)[:, :, 64 * h:64 * h + 64]
                        S.add("dve", lambda e, o=o_ap, i=i_ap: e.tensor_copy(out=o, in_=i),
                              writes=[("Vp", buf, b0 // 8, h)], banks=[b])

                tjobs = [(lambda b0=b0: t_job(b0)) for b0 in range(0, d * (nb + 1), 8)]
                if u == 0:
                    for ti, (t0, n) in enumerate(tiles):
                        jobs.append(lambda ti=ti, t0=t0, n=n: vt_job(ti, t0, n))
                        jobs.append(lambda ti=ti, t0=t0, n=n: k_job(ti, t0, n))
                        if t0 >= 0:
                            jobs.append(lambda w=t0 // 512: q_job(w))
                    jobs += tjobs
                elif u == 1:
                    own = [(ti, t0, n) for ti, (t0, n) in enumerate(tiles) if t0 >= 0]
                    halo = [(ti, t0, n) for ti, (t0, n) in enumerate(tiles) if t0 < 0]
                    for (ti, t0, n) in own:
                        jobs.append(lambda ti=ti, t0=t0, n=n: vt_job(ti, t0, n))
                    for (ti, t0, n) in own:
                        jobs.append(lambda ti=ti, t0=t0, n=n: k_job(ti, t0, n))
                    for w in range(4):
                        jobs.append(lambda w=w: q_job(w))
                    for w in range(4):
                        jobs.append(lambda w=w: g_job(w))
                    for (ti, t0, n) in reversed(halo):
                        jobs.append(lambda ti=ti, t0=t0, n=n: vt_job(ti, t0, n))
                    for (ti, t0, n) in reversed(halo):
                        jobs.append(lambda ti=ti, t0=t0, n=n: k_job(ti, t0, n))
                    jobs += tjobs
                    return jobs
                else:
                    for ti, (t0, n) in enumerate(tiles):
                        jobs.append(lambda ti=ti, t0=t0, n=n: vt_job(ti, t0, n))
                    jobs += tjobs
                    for ti, (t0, n) in enumerate(tiles):
                        jobs.append(lambda ti=ti, t0=t0, n=n: k_job(ti, t0, n))
                    for w in range(4):
                        jobs.append(lambda w=w: q_job(w))
                if ui == 1:
                    for w in range(4):
                        jobs.append(lambda w=w: g_job(w))
                return jobs

            def step(u):
                hp, ui = divmod(u, 3)
                g = GORD[ui]
                d = DILS[g]
                nb = 16 // d
                buf = u % 2
                ntile = len(ktiles(d))
                load_unit(u + 2)
                qres = [("QT", buf, w) for w in range(4)]
                kres = [("KT", buf, ti) for ti in range(ntile)]

                def sbank_desc(j):
                    slots = []
                    for q in (2 * j, 2 * j + 1):
                        r, qb = q // nb, q % nb
                        kprev = r * (nb + 1) + qb
                        slots.append((kprev, q))
                        slots.append((kprev + 1, q))
                    return slots

                def qk(j, h):
                    slots = sbank_desc(j)
                    bank = SB[h][j % 2]
                    groups = []
                    s = 0
                    while s < 4:
                        if s + 1 < 4 and slots[s + 1][0] == slots[s][0] and slots[s + 1][1] == slots[s][1] + 1:
                            groups.append((s, 2))
                            s += 2
                        else:
                            groups.append((s, 1))
                            s += 1

                    def fn(e):
                        inst = None
                        for (s0, cnt) in groups:
                            kblk, q = slots[s0]
                            inst = e.matmul(ps[bank][:, s0 * 128:(s0 + cnt) * 128],
                                            KT[buf][64 * h:64 * h + 64, kblk * 128:(kblk + 1) * 128],
                                            QT[buf][64 * h:64 * h + 64, q * 128:(q + cnt) * 128],
                                            start=True, stop=True)
                        return inst
                    S.add("pe", fn, reads=qres + kres, banks=[bank])

                def softmax_part(j, h):
                    bank = SB[h][j % 2]
                    pt = PT[h][j % 2]
                    if nb == 1:
                        mvar = 2
                    else:
                        mvar = 0 if (2 * j) % nb == 0 else 1
                    S.add("act", lambda e: e.activation(out=pt, in_=ps[bank][:, :], func=AF.Exp),
                          writes=[("PT", h, j % 2)], banks=[bank])
                    S.add("dve", lambda e: e.tensor_tensor(out=pt, in0=pt, in1=mk[:, mvar * 512:(mvar + 1) * 512],
                                                           op=ALU.mult),
                          reads=["mk", ("PT", h, j % 2)], writes=[("PT", h, j % 2)])

                def pv(j, h):
                    slots = sbank_desc(j)
                    pt = PT[h][j % 2]
                    ub = UB[h]

                    def fn(e):
                        inst = None
                        for s in range(4):
                            kblk, q = slots[s]
                            c0 = (q % 4) * 128
                            inst = e.matmul(ps[ub][:, c0:c0 + 128], Vp[buf][:, kblk, h, :], pt[:, s * 128:(s + 1) * 128],
                                            start=(s % 2 == 0), stop=(s % 2 == 1))
                        return inst
                    vres = sorted({("Vp", buf, slots[s][0] // 8, h) for s in range(4)})
                    S.add("pe", fn, reads=vres + [("PT", h, j % 2)], banks=[ub])

                def uevac(m, h):
                    ub = UB[h]
                    L = TOK // d
                    if d == 1:
                        o_ap = Uacc[:, h, 512 * m:512 * (m + 1)]
                        i_ap = ps[ub][:, :]
                    elif L >= 512:
                        r = (512 * m) // L
                        l0 = (512 * m) % L
                        o_ap = Uacc[:, h, :].rearrange("p (l r) -> p r l", r=d)[:, r, l0:l0 + 512]
                        i_ap = ps[ub][:, :]
                    else:
                        r0 = (512 * m) // L
                        nr = 512 // L
                        o_ap = Uacc[:, h, :].rearrange("p (l r) -> p r l", r=d)[:, r0:r0 + nr, :]
                        i_ap = ps[ub][:, :].rearrange("p (r l) -> p r l", r=nr)
                    allu = [("Uacc", h, mm) for mm in range(4)]
                    if ui == 0:
                        S.add("dve", lambda e: e.tensor_copy(out=o_ap, in_=i_ap),
                              writes=allu, banks=[ub])
                    else:
                        S.add("dve", lambda e: e.tensor_tensor(out=o_ap, in0=o_ap, in1=i_ap, op=ALU.add),
                              reads=allu, writes=[("Uacc", h, m)], banks=[ub])

                def finalize_piece(m):
                    sl = slice(512 * m, 512 * (m + 1))
                    rzm = rz[:, m % 2, :]
                    zs = ((Uacc[64:128, 0, sl], rzm[0:64, :], 0), (Uacc[0:64, 1, sl], rzm[64:128, :], 1))
                    for (z, ro, h) in zs:
                        S.add("act", lambda e, z=z: e.activation(out=z, in_=z, func=AF.Ln),
                              reads=[("Uacc", h, m)], writes=[("Uacc", h, m)])
                        S.add("act", lambda e, z=z, ro=ro: e.activation(out=ro, in_=z, func=AF.Exp, scale=-1.0),
                              reads=[("Uacc", h, m)], writes=[("rz", h, m % 2)])
                    S.add("pool", lambda e: e.tensor_tensor(out=rzm, in0=rzm, in1=gatt[:, sl], op=ALU.mult),
                          reads=[("rz", 0, m % 2), ("rz", 1, m % 2), ("gatt", m)], writes=[("rz", 0, m % 2), ("rz", 1, m % 2)])
                    S.add("pool", lambda e: e.tensor_tensor(out=aT[0:64, hp, sl], in0=Uacc[0:64, 0, sl], in1=rzm[0:64, :], op=ALU.mult),
                          reads=[("Uacc", 0, m), ("rz", 0, m % 2)], writes=[("aT", hp, 0, m)])
                    S.add("pool", lambda e: e.tensor_tensor(out=aT[64:128, hp, sl], in0=Uacc[64:128, 1, sl], in1=rzm[64:128, :], op=ALU.mult),
                          reads=[("Uacc", 1, m), ("rz", 1, m % 2)], writes=[("aT", hp, 1, m)])

                jobs = proj_jobs(u + 1) if u + 1 < 12 else []
                cuts = [(len(jobs) * j) // 8 for j in range(9)]
                pending = list(deferred)
                del deferred[:]
                for h in range(2):
                    qk(0, h)
                for j in range(8):
                    for h in range(2):
                        softmax_part(j, h)
                    if ui == 2 and j >= 2 and j % 2 == 0:
                        finalize_piece(j // 2 - 1)
                    if j + 1 < 8:
                        for h in range(2):
                            qk(j + 1, h)
                    if j == 0 and pending:
                        for fz in pending:
                            fz()
                    for job in jobs[cuts[j]:cuts[j + 1]]:
                        job()
                    for h in range(2):
                        pv(j, h)
                    if j % 2 == 1:
                        for h in range(2):
                            uevac(j // 2, h)
                        if ui == 2 and j == 7:
                            deferred.append(lambda: finalize_piece(3))

            deferred = []
            for job in proj_jobs(0):
                job()
            for u in range(12):
                step(u)
            for fz in deferred:
                fz()
            if debug:
                S.add("sp", lambda e: [e.dma_start(out=dbg_a[:, :], in_=aT[:].rearrange("p a t -> p (a t)"))],
                      reads=[("aT", hp, h, m) for hp in range(4) for h in range(2) for m in range(4)], dma="dbg")
            S.emit()

        es23 = ExitStack()
        uT = es23.enter_context(nc.sbuf_tensor("uT", [128, 8, TOK], BF16))
        bank_rot = [0]

        def next_bank():
            b = bank_rot[0]
            bank_rot[0] = (b + 1) % 8
            return b

        with ExitStack() as p2:
            def sb2(name, shape, dt):
                return p2.enter_context(nc.sbuf_tensor(name, shape, dt))
            hS = sb2("hS", [128, TOK + 2], F32)
            chS = sb2("chS", [128, TOK + 2], F32)
            acc = sb2("acc", [128, TOK], F32)
            sg = sb2("sg", [128, TOK], F32)
            ttiles = [(TOK - 2, 2, 0)] + [(TOK + 512 * w, 512, 2 + 512 * w) for w in range(4)]
            allch = [("chS", d0) for (_, _, d0) in ttiles]
            allacc = [("acc", w) for w in range(4)]
            for cc in range(8):
                u = 12 + cc
                load_unit(u + 2)
                W = wv(u, 512)
                wres = ("wr", u % 3)
                for (c0, n, d0) in ttiles:
                    b = next_bank()
                    proj(b, lambda kc: W[:, kc, 0:128], lambda kc, c0=c0, n=n: xT[:, kc, c0:c0 + n], n,
                         reads=[wres] + xres(c0, n))
                    S.add("act", lambda e, b=b, n=n, d0=d0: e.activation(out=hS[:, d0:d0 + n], in_=ps[b][:, 0:n], func=AF.Copy),
                          writes=[("hS", d0)], banks=[b])
                for (c0, n, d0) in ttiles:
                    b = next_bank()
                    proj(b, lambda kc: W[:, kc, 256:384], lambda kc, c0=c0, n=n: xT[:, kc, c0:c0 + n], n,
                         reads=[wres] + xres(c0, n))
                    S.add("dve", lambda e, b=b, n=n, d0=d0: e.tensor_tensor(out=chS[:, d0:d0 + n], in0=ps[b][:, 0:n],
                                                                            in1=hS[:, d0:d0 + n], op=ALU.mult),
                          reads=[("hS", d0)], writes=[("chS", d0)], banks=[b])
                w0 = cst[:, cc * 3 + 0:cc * 3 + 1]
                w1 = cst[:, cc * 3 + 1:cc * 3 + 2]
                w2 = cst[:, cc * 3 + 2:cc * 3 + 3]
                S.add("act", lambda e, w2=w2: e.activation(out=acc[:, :], in_=chS[:, 2:TOK + 2], func=AF.Copy, scale=w2),
                      reads=allch + ["cst"], writes=allacc)
                S.add("dve", lambda e, w1=w1: e.scalar_tensor_tensor(out=acc[:, :], in0=chS[:, 1:TOK + 1], scalar=w1,
                                                                     in1=acc[:, :], op0=ALU.mult, op1=ALU.add),
                      reads=allch + allacc + ["cst"], writes=allacc)
                S.add("dve", lambda e, w0=w0: e.scalar_tensor_tensor(out=acc[:, :], in0=chS[:, 0:TOK], scalar=w0,
                                                                     in1=acc[:, :], op0=ALU.mult, op1=ALU.add),
                      reads=allch + allacc + ["cst"], writes=allacc)
                for w in range(4):
                    b = next_bank()
                    proj(b, lambda kc: W[:, kc, 384:512], lambda kc, w=w: xT[:, kc, TOK + 512 * w:TOK + 512 * (w + 1)], 512,
                         reads=[wres, ("xT", 4 + w)])
                    S.add("act", lambda e, b=b, w=w: e.activation(out=sg[:, 512 * w:512 * (w + 1)], in_=ps[b][:, :], func=AF.Silu),
                          writes=[("sg", w)], banks=[b])
                for w in range(4):
                    b = next_bank()
                    sl = slice(512 * w, 512 * (w + 1))
                    proj(b, lambda kc: W[:, kc, 128:256], lambda kc, w=w: xT[:, kc, TOK + 512 * w:TOK + 512 * (w + 1)], 512,
                         reads=[wres, ("xT", 4 + w)])
                    S.add("dve", lambda e, b=b, sl=sl: e.tensor_tensor(out=acc[:, sl], in0=ps[b][:, :], in1=acc[:, sl], op=ALU.mult),
                          reads=[("acc", w)], writes=[("acc", w)], banks=[b])
                    S.add("dve", lambda e, sl=sl, cc=cc: e.tensor_tensor(out=uT[:, cc, sl], in0=acc[:, sl], in1=sg[:, sl], op=ALU.mult),
                          reads=[("acc", w), ("sg", w)], writes=[("uT", cc, w)])
            if debug:
                S.add("sp", lambda e: [e.dma_start(out=dbg_u[:, :], in_=uT[:].rearrange("p a t -> p (a t)"))],
                      reads=[("uT", cc, w) for cc in range(8) for w in range(4)], dma="dbg")
            S.emit()

        with ExitStack() as p3:
            def sb3(name, shape, dt):
                return p3.enter_context(nc.sbuf_tensor(name, shape, dt))
            gcS = [sb3(f"gcS{i}", [128, 512], F32) for i in range(2)]
            gaS = [sb3(f"gaS{i}", [128, 512], F32) for i in range(2)]
            mS = [sb3(f"mS{i}", [128, 512], F32) for i in range(2)]
            it = 0
            for j in range(8):
                u = 20 + j
                load_unit(u + 2)
                Wg = wr[u % 3][:, 0:2048].rearrange("p (k c) -> p k c", c=256)
                Wco = wr[u % 3][:, 2048:3072].rearrange("p (k c) -> p k c", c=128)
                Wao = wr[u % 3][:, 3072:3584].rearrange("p (k c) -> p k c", c=128)
                wres = ("wr", u % 3)
                for w in range(4):
                    i2 = it % 2
                    it += 1
                    tsl = slice(512 * w, 512 * (w + 1))
                    xsl = slice(TOK + 512 * w, TOK + 512 * (w + 1))
                    b = next_bank()
                    proj(b, lambda kc: Wg[:, kc, 0:128], lambda kc, xsl=xsl: xT[:, kc, xsl], 512, reads=[wres, ("xT", 4 + w)])
                    S.add("act", lambda e, b=b, i2=i2, j=j: e.activation(out=gcS[i2][:, :], in_=ps[b][:, :], func=AF.Sigmoid,
                                                                         bias=cst[:, 24 + j:25 + j]),
                          reads=["cst"], writes=[("gcS", i2)], banks=[b])
                    b = next_bank()
                    proj(b, lambda kc: Wg[:, kc, 128:256], lambda kc, xsl=xsl: xT[:, kc, xsl], 512, reads=[wres, ("xT", 4 + w)])
                    S.add("act", lambda e, b=b, i2=i2, j=j: e.activation(out=gaS[i2][:, :], in_=ps[b][:, :], func=AF.Sigmoid,
                                                                         bias=cst[:, 32 + j:33 + j]),
                          reads=["cst"], writes=[("gaS", i2)], banks=[b])
                    b = next_bank()
                    proj(b, lambda kc: Wco[:, kc, :], lambda kc, tsl=tsl: uT[:, kc, tsl], 512,
                         reads=[wres] + [("uT", cc, w) for cc in range(8)])
                    S.add("dve", lambda e, b=b, i2=i2: e.tensor_tensor(out=mS[i2][:, :], in0=ps[b][:, :], in1=gcS[i2][:, :], op=ALU.mult),
                          reads=[("gcS", i2)], writes=[("mS", i2)], banks=[b])
                    b = next_bank()
                    proj(b, lambda kc: Wao[:, kc, :], lambda kc, tsl=tsl: aT[:, kc, tsl], 512, nk=4,
                         reads=[wres] + [("aT", hp, h, w) for hp in range(4) for h in range(2)])
                    S.add("dve", lambda e, b=b, i2=i2: e.tensor_tensor(out=gaS[i2][:, :], in0=ps[b][:, :], in1=gaS[i2][:, :], op=ALU.mult),
                          reads=[("gaS", i2)], writes=[("gaS", i2)], banks=[b])
                    S.add("dve", lambda e, i2=i2, j=j, tsl=tsl: e.tensor_tensor(out=mT[:, j, tsl], in0=mS[i2][:, :], in1=gaS[i2][:, :], op=ALU.add),
                          reads=[("mS", i2), ("gaS", i2)], writes=[("mT", j, w)])
            if debug:
                S.add("sp", lambda e: [e.dma_start(out=dbg_m[:, :], in_=low[:, :])],
                      reads=[("mT", j, w) for j in range(8) for w in range(4)], dma="dbg")
            S.emit()

        es23.close()
        es123.close()
        with ExitStack() as p4:
            def sb4(name, shape, dt):
                return p4.enter_context(nc.sbuf_tensor(name, shape, dt))
            gb = sb4("gb_sb", [128, 2 * DM], F32)
            NY = 4
            xt = [sb4(f"xt{i}", [128, DM], F32) for i in range(3)]
            ys = [sb4(f"ys{i}", [128, DM], F32) for i in range(NY)]
            ob = [sb4(f"ob{i}", [128, DM], F32) for i in range(3)]
            sa = [sb4(f"sa{i}", [128, 64], F32) for i in range(6)]
            junk = sb4("junk", [128, DM], BF16)
            sd = [sb4(f"sd{i}", [128, 64], F32) for i in range(6)]
            wo = [wv(28 + hf, 512) for hf in range(2)]
            S.add("sp", lambda e: [e.dma_start(out=gb[:], in_=gb_d[:, :])], writes=["gb"], dma="gb")
            S.add("dve", lambda e: e.tensor_copy(out=pp[3][:, :], in_=gb[:, 0:DM]),
                  reads=["gb"], writes=["gbp"], banks=[6, 7])
            out_toks = []

            def load_xt(t):
                S.add("sp", lambda e: [e.dma_start(out=xt[t % 3][:], in_=xtok_d[128 * t:128 * (t + 1), :])],
                      writes=[("xt", t % 3)], dma=f"xt{t % 3}")

            load_xt(0)
            load_xt(1)

            def stage_a1(t):
                i3, iy, i6, pb = t % 3, t % NY, t % 6, t % 3
                if t + 2 < 16:
                    load_xt(t + 2)
                for hf in range(2):
                    proj(2 * pb + hf, lambda kc: mT[:, kc, 128 * t:128 * (t + 1)], lambda kc, hf=hf: wo[hf][:, kc, :], 512,
                         reads=[("mT", j, t // 4) for j in range(8)] + [("wr", (28 + hf) % 3)])
                S.add("dve", lambda e: e.scalar_tensor_tensor(
                    out=ys[iy][:, :], in0=xt[i3][:, :], scalar=ALPHA, in1=pp[pb][:, :], op0=ALU.mult, op1=ALU.add),
                    reads=[("xt", i3)], writes=[("ys", iy)], banks=[2 * pb, 2 * pb + 1])
                S.add("act", lambda e: e.activation(out=junk[:, :], in_=ys[iy][:, :], func=AF.Copy, scale=1.0 / DM,
                                                    accum_out=sa[i6][:, 0:1]),
                      reads=[("ys", iy)], writes=["junk", ("sa", i6, 0)])
                S.add("act", lambda e: e.activation(out=junk[:, :], in_=ys[iy][:, :], func=AF.Square, scale=DM ** -0.5,
                                                    accum_out=sa[i6][:, 1:2]),
                      reads=[("ys", iy)], writes=["junk", ("sa", i6, 1)])

            def stage_a2(t):
                i6 = t % 6
                S.add("dve", lambda e: e.tensor_tensor(out=sd[i6][:, 0:1], in0=sa[i6][:, 0:1], in1=sa[i6][:, 0:1], op=ALU.mult),
                      reads=[("sa", i6, 0)], writes=[("sd", i6, 0)])
                S.add("dve", lambda e: e.tensor_scalar(out=sd[i6][:, 4:5], in0=sa[i6][:, 0:1], scalar1=-1.0, scalar2=None, op0=ALU.mult),
                      reads=[("sa", i6, 0)], writes=[("sd", i6, 4)])
                S.add("dve", lambda e: e.tensor_tensor(out=sd[i6][:, 1:2], in0=sa[i6][:, 1:2], in1=sd[i6][:, 0:1], op=ALU.subtract),
                      reads=[("sa", i6, 1), ("sd", i6, 0)], writes=[("sd", i6, 1)])
                S.add("act", lambda e: e.activation(out=sa[i6][:, 2:3], in_=sd[i6][:, 1:2], func=AF.Sqrt, bias=LN_EPS),
                      reads=[("sd", i6, 1)], writes=[("sa", i6, 2)])

            def stage_b(t):
                i3, iy, i6 = t % 3, t % NY, t % 6
                S.add("dve", lambda e: e.reciprocal(out=sd[i6][:, 2:3], in_=sa[i6][:, 2:3]),
                      reads=[("sa", i6, 2)], writes=[("sd", i6, 2)])
                S.add("dve", lambda e: e.tensor_tensor(out=sd[i6][:, 3:4], in0=sd[i6][:, 4:5], in1=sd[i6][:, 2:3], op=ALU.mult),
                      reads=[("sd", i6, 4), ("sd", i6, 2)], writes=[("sd", i6, 3)])
                S.add("act", lambda e: e.activation(out=ob[i3][:, :], in_=ys[iy][:, :], func=AF.Identity,
                                                    scale=sd[i6][:, 2:3], bias=sd[i6][:, 3:4]),
                      reads=[("ys", iy), ("sd", i6, 2), ("sd", i6, 3)], writes=[("ob", i3)])

            def stage_d(t):
                i3 = t % 3
                S.add("dve", lambda e: e.tensor_tensor(out=ob[i3][:, :], in0=ob[i3][:, :], in1=pp[3][:, :], op=ALU.mult),
                      reads=[("ob", i3), "gbp"], writes=[("ob", i3)], banks=[6, 7])
                S.add("pool", lambda e: e.tensor_tensor(out=ob[i3][:, :], in0=ob[i3][:, :], in1=gb[:, DM:2 * DM], op=ALU.add),
                      reads=[("ob", i3), "gb"], writes=[("ob", i3)])
                tok = S.add("pool", lambda e: [e.dma_start(out=out_d[128 * t:128 * (t + 1), :], in_=ob[i3][:])],
                            reads=[("ob", i3)], dma=f"ob{i3}")
                out_toks.append(tok)

            for t in range(16 + 3):
                if t < 16:
                    stage_a1(t)
                if 0 <= t - 2 < 16:
                    stage_b(t - 2)
                if 0 <= t - 1 < 16:
                    stage_a2(t - 1)
                if 0 <= t - 3 < 16:
                    stage_d(t - 3)
            finals = out_toks[-3:]
            if debug:
                finals = finals + [("d:dbg", S.dma_cnt["d:dbg"], "sp", True)]
            S.emit(final_waits=finals)
    return nc


_NC_CACHE = {}


def _get_nc(debug=False):
    if debug not in _NC_CACHE:
        _NC_CACHE[debug] = build(debug)
    return _NC_CACHE[debug]


def make_in_maps(x, w_in, conv_w, w_conv_out, w_att_out, b_gate, w_o, ln_g, ln_b):
    x2 = np.asarray(x, np.float32).reshape(SEQ, DM)
    ws, cst, gb = _prep_weights(w_in, conv_w, w_conv_out, w_att_out, b_gate, w_o, ln_g, ln_b)
    in_maps = []
    for c in range(NCORES):
        own = x2[c * TOK:(c + 1) * TOK]
        halo = x2[(c - 1) * TOK:c * TOK] if c > 0 else np.zeros((TOK, DM), np.float32)
        xe = np.concatenate([halo, own], axis=0)
        xT = np.ascontiguousarray(xe.T).reshape(8, 128, XC)
        in_maps.append({"xT": xT, "xtok": np.ascontiguousarray(own), "ws": ws,
                        "cst": cst, "gb": gb, "mk": _masks(c)})
    return in_maps


def kernel(x, w_in, conv_w, w_conv_out, w_att_out, b_gate, w_o, ln_g, ln_b):
    nc = _get_nc(False)
    in_maps = make_in_maps(x, w_in, conv_w, w_conv_out, w_att_out, b_gate, w_o, ln_g, ln_b)
    res = run_bass_kernel_spmd(nc, in_maps, core_ids=list(range(NCORES)))
    out = np.concatenate([r["out"] for r in res.results], axis=0)
    return out.reshape(1, SEQ, DM).astype(np.float32)
```

```python
import numpy as np
import concourse.bass as bass
import concourse.mybir as mybir
from concourse.bass_utils import run_bass_kernel_spmd

F32 = mybir.dt.float32
BF16 = mybir.dt.bfloat16
AF = mybir.ActivationFunctionType
ALU = mybir.AluOpType

NCORES = 8
SEQ = 16384
DM = 1024
TOK = SEQ // NCORES
XC = 2 * TOK
DILS = (1, 4, 16)
GORD = (1, 2, 0)
ALPHA = 2.0 ** 0.25
LN_EPS = 1e-5
NUNITS = 30
UW = 4096

Q0, K0, V0, GA0, H0, B0, C0, GC0, GL0 = 0, 1536, 3072, 4608, 5120, 6144, 7168, 8192, 9216


class Sched:
    ENGS = ("pe", "act", "dve", "pool", "sp")

    def __init__(self, nc, esems, dma_sems):
        self.nc = nc
        self.semobj = {"e:" + e: s for e, s in esems.items()}
        self.free_dma = list(dma_sems)
        self.ops = {e: [] for e in self.ENGS}
        self.count = {e: 0 for e in self.ENGS}
        self.lastw = {}
        self.readers = {}
        self.bank_last = {}
        self.waited = {e: {} for e in self.ENGS}
        self.dma_cnt = {}
        self.n_ops = 0

    def _slot(self, slot):
        key = "d:" + slot
        if key not in self.semobj:
            self.semobj[key] = self.free_dma.pop()
            self.dma_cnt[key] = 0
        return key

    def add(self, eng, fn, reads=(), writes=(), banks=(), dma=None, ndma=1):
        is_dma = dma is not None
        deps = {}

        def need(tok, kind):
            if tok is None:
                return
            semkey, val, teng, tdma = tok
            if not tdma and teng == eng and not is_dma and eng == "pe":
                return
            if deps.get(semkey, 0) < val:
                deps[semkey] = val

        for r in reads:
            need(self.lastw.get(r), "raw")
        for w in writes:
            need(self.lastw.get(w), "waw")
            for t in self.readers.get(w, ()):
                need(t, "war")
        for b in banks:
            t = self.bank_last.get(b)
            if t is not None and t[2] != eng:
                need(t, "bank")

        if is_dma:
            key = self._slot(dma)
            self.dma_cnt[key] += 16 * ndma
            tok = (key, self.dma_cnt[key], eng, True)
        else:
            self.count[eng] += 1
            tok = ("e:" + eng, self.count[eng], eng, False)
        for r in reads:
            self.readers.setdefault(r, []).append(tok)
        for w in writes:
            self.lastw[w] = tok
            self.readers[w] = []
        for b in banks:
            self.bank_last[b] = tok
        self.ops[eng].append((fn, deps, tok))
        self.n_ops += 1
        return tok

    def emit(self, final_waits=()):
        nc = self.nc
        with nc.Block() as block:
            deco = {"pe": block.tensor, "act": block.scalar, "dve": block.vector,
                    "pool": block.gpsimd, "sp": block.sync}
            for eng in self.ENGS:
                oplist = self.ops[eng]
                fw = list(final_waits) if eng == "sp" else []
                if not oplist and not fw:
                    continue

                def body(e, oplist=oplist, eng=eng, fw=fw):
                    waited = self.waited[eng]
                    for fn, deps, tok in oplist:
                        for semkey, val in deps.items():
                            if waited.get(semkey, 0) >= val:
                                continue
                            e.wait_ge(self.semobj[semkey], val)
                            waited[semkey] = val
                        res = fn(e)
                        if tok[3]:
                            for inst in res:
                                inst.then_inc(self.semobj[tok[0]], 16)
                        else:
                            res.then_inc(self.semobj[tok[0]], 1)
                    for semkey, val, _, _ in fw:
                        if waited.get(semkey, 0) >= val:
                            continue
                        e.wait_ge(self.semobj[semkey], val)
                        waited[semkey] = val

                deco[eng](body)
        self.ops = {e: [] for e in self.ENGS}


def _unit(wc):
    k, n = wc.shape
    nk = k // 128
    return np.ascontiguousarray(wc.reshape(nk, 128, n).transpose(1, 0, 2)).reshape(128, nk * n)


def _prep_weights(w_in, conv_w, w_conv_out, w_att_out, b_gate, w_o, ln_g, ln_b):
    W = np.asarray(w_in[0], np.float32)
    wco = np.asarray(w_conv_out[0], np.float32)
    wao = np.asarray(w_att_out[0], np.float32)
    ws = np.zeros((NUNITS, 128, UW), np.float32)
    zeros128 = np.zeros((DM, 128), np.float32)
    for hp in range(4):
        for i, g in enumerate(GORD):
            c = g * 512 + hp * 128
            cols = [W[:, Q0 + c:Q0 + c + 128], W[:, K0 + c:K0 + c + 128], W[:, V0 + c:V0 + c + 128],
                    W[:, GA0 + hp * 128:GA0 + hp * 128 + 128] if i == 1 else zeros128]
            ws[hp * 3 + i] = _unit(np.concatenate(cols, axis=1))
    for cc in range(8):
        c = cc * 128
        cols = [W[:, H0 + c:H0 + c + 128], W[:, B0 + c:B0 + c + 128],
                W[:, C0 + c:C0 + c + 128], W[:, GC0 + c:GC0 + c + 128]]
        ws[12 + cc] = _unit(np.concatenate(cols, axis=1))
    for j in range(8):
        c = j * 128
        gw = np.concatenate([W[:, GL0 + c:GL0 + c + 128], W[:, GL0 + 1024 + c:GL0 + 1024 + c + 128]], axis=1)
        ws[20 + j, :, 0:2048] = _unit(gw)
        ws[20 + j, :, 2048:3072] = _unit(wco[:, c:c + 128])
        ws[20 + j, :, 3072:3584] = _unit(wao[:, c:c + 128])
    wof = np.asarray(w_o[0], np.float32)
    ws[28] = _unit(wof[:, 0:512])
    ws[29] = _unit(wof[:, 512:1024])
    cst = np.zeros((128, 40), np.float32)
    cw = np.asarray(conv_w[0], np.float32)
    cst[:, 0:24] = cw.reshape(3, 8, 128).transpose(2, 1, 0).reshape(128, 24)
    cst[:, 24:40] = np.asarray(b_gate[0], np.float32).reshape(16, 128).T
    gb = np.concatenate([np.broadcast_to(np.asarray(ln_g[0], np.float32), (128, DM)),
                         np.broadcast_to(np.asarray(ln_b[0], np.float32), (128, DM))], axis=1)
    return ws, cst, np.ascontiguousarray(gb)


def _masks(core):
    j = np.arange(128)[:, None]
    i = np.arange(128)[None, :]
    p = (j >= i).astype(np.float32)
    c = (j <= i).astype(np.float32)
    ph = p * (1.0 if core > 0 else 0.0)
    ident = np.eye(128, dtype=np.float32)
    return np.ascontiguousarray(np.concatenate([ph, c, p, c, p, c, p, c, ph, c, ph, c, ident], axis=1))


MKW = 1536 + 128


def build(debug=False):
    from contextlib import ExitStack
    nc = bass.Bass("TRN2", target_bir_lowering=False)
    xT_d = nc.dram_tensor("xT", [8, 128, XC], F32, kind="ExternalInput").ap()
    xtok_d = nc.dram_tensor("xtok", [TOK, DM], F32, kind="ExternalInput").ap()
    ws_d = nc.dram_tensor("ws", [NUNITS, 128, UW], F32, kind="ExternalInput").ap()
    cst_d = nc.dram_tensor("cst", [128, 40], F32, kind="ExternalInput").ap()
    gb_d = nc.dram_tensor("gb", [128, 2 * DM], F32, kind="ExternalInput").ap()
    mk_d = nc.dram_tensor("mk", [128, MKW], F32, kind="ExternalInput").ap()
    out_d = nc.dram_tensor("out", [TOK, DM], F32, kind="ExternalOutput").ap()
    if debug:
        dbg_a = nc.dram_tensor("dbg_a", [128, 4 * TOK], BF16, kind="ExternalOutput").ap()
        dbg_u = nc.dram_tensor("dbg_u", [128, 8 * TOK], BF16, kind="ExternalOutput").ap()
        dbg_m = nc.dram_tensor("dbg_m", [128, 8 * TOK], BF16, kind="ExternalOutput").ap()

    with ExitStack() as es:
        def sb(name, shape, dt):
            return es.enter_context(nc.sbuf_tensor(name, shape, dt))

        mk = sb("mk_bf", [128, MKW], BF16)
        ident = mk[:, 1536:1664]
        cst = sb("cst_sb", [128, 40], F32)
        wr = [sb(f"wr{i}", [128, UW], BF16) for i in range(3)]
        low = sb("low32", [128, 16384], BF16)
        mT = low[:, :].rearrange("p (k t) -> p k t", k=8)
        es123 = ExitStack()
        xT = es123.enter_context(nc.sbuf_tensor("xT_bf", [128, 8, XC], BF16))
        aT = es123.enter_context(nc.sbuf_tensor("aT", [128, 4, TOK], BF16))
        pp = [es.enter_context(nc.psum_tensor(f"pp{i}", [128, 1024], F32)) for i in range(4)]
        ps = [pp[i // 2][:, (i % 2) * 512:(i % 2 + 1) * 512] for i in range(8)]
        esems = {e: es.enter_context(nc.semaphore("sem_" + e)) for e in Sched.ENGS}
        dsems = [es.enter_context(nc.semaphore(f"dsem{i}")) for i in range(32)]
        S = Sched(nc, esems, dsems)

        def load_unit(u):
            if u >= NUNITS:
                return
            buf = wr[u % 3]
            if u < 12 and u % 3 != 1:
                o_ap = buf[:, :].rearrange("p (k c) -> p k c", c=512)[:, :, 0:384]
                i_ap = ws_d[u].rearrange("p (k c) -> p k c", c=512)[:, :, 0:384]
            else:
                o_ap, i_ap = buf[:], ws_d[u]
            S.add("pool", lambda e: [e.dma_start(out=o_ap, in_=i_ap)],
                  writes=[("wr", u % 3)], dma=f"wr{u % 3}")

        def wv(u, width):
            return wr[u % 3][:, 0:8 * width].rearrange("p (k c) -> p k c", c=width)

        load_unit(0)
        xsrc = xT_d.rearrange("k p c -> p k c")
        xpieces = [(3, 1536, 2048), (4, 2048, 2560), (5, 2560, 3072), (6, 3072, 3584), (7, 3584, 4096),
                   (2, 1024, 1536), (1, 512, 1024), (0, 0, 512)]
        for (nm, c0, c1) in xpieces:
            S.add("pool", lambda e, c0=c0, c1=c1: [e.dma_start(out=xT[:, :, c0:c1], in_=xsrc[:, :, c0:c1])],
                  writes=[("xT", nm)], dma=f"xT{nm}")
            if nm == 4:
                S.add("pool", lambda e: [e.dma_start(out=mk[:], in_=mk_d[:, :])], writes=["mk"], dma="mk")
            if nm == 7:
                load_unit(1)
        S.add("sp", lambda e: [e.dma_start(out=cst[:], in_=cst_d[:, :])], writes=["cst"], dma="cst")

        def xres(c0, n):
            out = []
            for (nm, a, b) in xpieces:
                if a < c0 + n and c0 < b:
                    out.append(("xT", nm))
            return out

        def proj(bank, lhs_fn, rhs_fn, n, nk=8, reads=()):
            lhs = [lhs_fn(kc) for kc in range(nk)]
            rhs = [rhs_fn(kc) for kc in range(nk)]

            def fn(e):
                inst = None
                for kc in range(nk):
                    inst = e.matmul(ps[bank][:, 0:n], lhs[kc], rhs[kc],
                                    start=(kc == 0), stop=(kc == nk - 1))
                return inst
            S.add("pe", fn, reads=list(reads), banks=[bank])

        with ExitStack() as p1:
            def sb1(name, shape, dt):
                return p1.enter_context(nc.sbuf_tensor(name, shape, dt))
            Vp1t = sb1("Vp1", [128, 8192], BF16)
            Vp = [low[:, 0:8192].rearrange("p (b h c) -> p b h c", h=2, c=128),
                  Vp1t[:, :].rearrange("p (b h c) -> p b h c", h=2, c=128)]
            PT = [[low[:, 8192 + (2 * h + i) * 512:8192 + (2 * h + i + 1) * 512] for i in range(2)] for h in range(2)]
            QT = [low[:, 10240:12288], sb1("QT1", [128, TOK], BF16)]
            KT = [low[:, 12288:16384], sb1("KT1", [128, 2 * TOK], BF16)]
            VT = sb1("VT0", [128, XC], BF16)
            Uacc = sb1("Uacc", [128, 2, TOK], F32)
            gatt = sb1("gatt", [128, TOK], F32)
            rz = sb1("rz", [128, 2, 512], F32)

            S.add("dve", lambda e: e.memset(low[:, 0:8192], 1.0),
                  writes=[("Vp", 0, gi, h) for gi in range(4) for h in range(2)])
            S.add("dve", lambda e: e.memset(Vp1t[:, :], 1.0),
                  writes=[("Vp", 1, gi, h) for gi in range(4) for h in range(2)])

            pj_rot = [0]

            def next_pj():
                b = pj_rot[0]
                pj_rot[0] = (b + 1) % 2
                return b

            SB = {0: [2, 3], 1: [4, 5]}
            UB = {0: 6, 1: 7}

            def ktiles(d):
                tiles = []
                t0 = -128 * d
                while t0 < 0:
                    n = min(512, -t0)
                    tiles.append((t0, n))
                    t0 += n
                return tiles + [(512 * w, 512) for w in range(4)]

            def proj_jobs(u):
                hp, ui = divmod(u, 3)
                g = GORD[ui]
                d = DILS[g]
                nb = 16 // d
                buf = u % 2
                W = wv(u, 512)
                wres = ("wr", u % 3)
                jobs = []
                tiles = ktiles(d)
                klen = (nb + 1) * 128

                def vt_job(ti, t0, n):
                    b = next_pj()
                    proj(b, lambda kc: W[:, kc, 256:384], lambda kc: xT[:, kc, TOK + t0:TOK + t0 + n], n,
                         reads=[wres] + xres(TOK + t0, n))
                    S.add("act", lambda e: e.activation(out=VT[:, TOK + t0:TOK + t0 + n], in_=ps[b][:, 0:n], func=AF.Copy),
                          writes=[("VT", ti)], banks=[b])

                def k_job(ti, t0, n):
                    b = next_pj()
                    proj(b, lambda kc: W[:, kc, 128:256], lambda kc: xT[:, kc, TOK + t0:TOK + t0 + n], n,
                         reads=[wres] + xres(TOK + t0, n))
                    lk0 = (t0 + 128 * d) // d
                    cnt = n // d
                    if d == 1:
                        o_ap = KT[buf][:, lk0:lk0 + cnt]
                        i_ap = ps[b][:, 0:n]
                    else:
                        o_ap = KT[buf][:, 0:d * klen].rearrange("p (r l) -> p r l", r=d)[:, :, lk0:lk0 + cnt]
                        i_ap = ps[b][:, 0:n].rearrange("p (l r) -> p r l", r=d)
                    S.add("act", lambda e: e.activation(out=o_ap, in_=i_ap, func=AF.Copy),
                          writes=[("KT", buf, ti)], banks=[b])

                def q_job(w):
                    b = next_pj()
                    proj(b, lambda kc: W[:, kc, 0:128], lambda kc: xT[:, kc, TOK + 512 * w:TOK + 512 * (w + 1)], 512,
                         reads=[wres, ("xT", 4 + w)])
                    lk0 = 512 * w // d
                    cnt = 512 // d
                    if d == 1:
                        o_ap = QT[buf][:, lk0:lk0 + cnt]
                        i_ap = ps[b][:, 0:512]
                    else:
                        o_ap = QT[buf][:, :].rearrange("p (r l) -> p r l", r=d)[:, :, lk0:lk0 + cnt]
                        i_ap = ps[b][:, 0:512].rearrange("p (l r) -> p r l", r=d)
                    S.add("act", lambda e: e.activation(out=o_ap, in_=i_ap, func=AF.Copy, scale=0.125),
                          writes=[("QT", buf, w)], banks=[b])

                def g_job(w):
                    b = next_pj()
                    proj(b, lambda kc: W[:, kc, 384:512], lambda kc: xT[:, kc, TOK + 512 * w:TOK + 512 * (w + 1)], 512,
                         reads=[wres, ("xT", 4 + w)])
                    S.add("act", lambda e: e.activation(out=gatt[:, 512 * w:512 * (w + 1)], in_=ps[b][:, 0:512], func=AF.Silu),
                          writes=[("gatt", w)], banks=[b])

                def t_job(b0):
                    nblk = d * (nb + 1)
                    nbk = min(8, nblk - b0)
                    b = next_pj()
                    pbf = ps[b][:, :].bitcast(BF16)

                    def fnt(e):
                        inst = None
                        for q in range(nbk):
                            vb = b0 + q
                            r, kb = vb // (nb + 1), vb % (nb + 1)
                            st = TOK - 128 * d + kb * 128 * d + r
                            inst = e.transpose(pbf[:, q * 128:(q + 1) * 128], VT[:, st:st + 127 * d + 1:d], ident)
                        return inst
                    S.add("pe", fnt, reads=["mk"] + [("VT", ti) for ti in range(len(tiles))], banks=[b])
                    for h in range(2):
                        o_ap = Vp[buf][:, b0:b0 + nbk, h, 64 * h:64 * h + 64]
                        i_ap = pbf[:, 0:nbk * 128].rearrange("p (q c) -> p q c", c=128)[:, :, 64 * h:64 * h + 64]
                        S.add("dve", lambda e, o=o_ap, i=i_ap: e.tensor_copy(out=o, in_=i),
                              writes=[("Vp", buf, b0 // 8, h)], banks=[b])

                tjobs = [(lambda b0=b0: t_job(b0)) for b0 in range(0, d * (nb + 1), 8)]
                if u == 0:
                    for ti, (t0, n) in enumerate(tiles):
                        jobs.append(lambda ti=ti, t0=t0, n=n: vt_job(ti, t0, n))
                        jobs.append(lambda ti=ti, t0=t0, n=n: k_job(ti, t0, n))
                        if t0 >= 0:
                            jobs.append(lambda w=t0 // 512: q_job(w))
                    jobs += tjobs
                elif u == 1:
                    own = [(ti, t0, n) for ti, (t0, n) in enumerate(tiles) if t0 >= 0]
                    halo = [(ti, t0, n) for ti, (t0, n) in enumerate(tiles) if t0 < 0]
                    for (ti, t0, n) in own:
                        jobs.append(lambda ti=ti, t0=t0, n=n: vt_job(ti, t0, n))
                    for (ti, t0, n) in own:
                        jobs.append(lambda ti=ti, t0=t0, n=n: k_job(ti, t0, n))
                    for w in range(4):
                        jobs.append(lambda w=w: q_job(w))
                    for w in range(4):
                        jobs.append(lambda w=w: g_job(w))
                    for (ti, t0, n) in reversed(halo):
                        jobs.append(lambda ti=ti, t0=t0, n=n: vt_job(ti, t0, n))
                    for (ti, t0, n) in reversed(halo):
                        jobs.append(lambda ti=ti, t0=t0, n=n: k_job(ti, t0, n))
                    jobs += tjobs
                    return jobs
                else:
                    for ti, (t0, n) in enumerate(tiles):
                        jobs.append(lambda ti=ti, t0=t0, n=n: vt_job(ti, t0, n))
                    for ti, (t0, n) in enumerate(tiles):
                        jobs.append(lambda ti=ti, t0=t0, n=n: k_job(ti, t0, n))
                    for w in range(4):
                        jobs.append(lambda w=w: q_job(w))
                    jobs += tjobs
                if ui == 1:
                    for w in range(4):
                        jobs.append(lambda w=w: g_job(w))
                return jobs

            def step(u):
                hp, ui = divmod(u, 3)
                g = GORD[ui]
                d = DILS[g]
                nb = 16 // d
                buf = u % 2
                ntile = len(ktiles(d))
                load_unit(u + 2)
                qres = [("QT", buf, w) for w in range(4)]
                kres = [("KT", buf, ti) for ti in range(ntile)]

                def sbank_desc(j):
                    slots = []
                    for q in (2 * j, 2 * j + 1):
                        r, qb = q // nb, q % nb
                        kprev = r * (nb + 1) + qb
                        slots.append((kprev, q))
                        slots.append((kprev + 1, q))
                    return slots

                def qk(j, h):
                    slots = sbank_desc(j)
                    bank = SB[h][j % 2]
                    groups = []
                    s = 0
                    while s < 4:
                        if s + 1 < 4 and slots[s + 1][0] == slots[s][0] and slots[s + 1][1] == slots[s][1] + 1:
                            groups.append((s, 2))
                            s += 2
                        else:
                            groups.append((s, 1))
                            s += 1

                    def fn(e):
                        inst = None
                        for (s0, cnt) in groups:
                            kblk, q = slots[s0]
                            inst = e.matmul(ps[bank][:, s0 * 128:(s0 + cnt) * 128],
                                            KT[buf][64 * h:64 * h + 64, kblk * 128:(kblk + 1) * 128],
                                            QT[buf][64 * h:64 * h + 64, q * 128:(q + cnt) * 128],
                                            start=True, stop=True)
                        return inst
                    S.add("pe", fn, reads=qres + kres, banks=[bank])

                def softmax_part(j, h):
                    bank = SB[h][j % 2]
                    pt = PT[h][j % 2]
                    if nb == 1:
                        mvar = 2
                    else:
                        mvar = 0 if (2 * j) % nb == 0 else 1
                    S.add("act", lambda e: e.activation(out=pt, in_=ps[bank][:, :], func=AF.Exp),
                          writes=[("PT", h, j % 2)], banks=[bank])
                    S.add("dve", lambda e: e.tensor_tensor(out=pt, in0=pt, in1=mk[:, mvar * 512:(mvar + 1) * 512],
                                                           op=ALU.mult),
                          reads=["mk", ("PT", h, j % 2)], writes=[("PT", h, j % 2)])

                def pv(j, h):
                    slots = sbank_desc(j)
                    pt = PT[h][j % 2]
                    ub = UB[h]

                    def fn(e):
                        inst = None
                        for s in range(4):
                            kblk, q = slots[s]
                            c0 = (q % 4) * 128
                            inst = e.matmul(ps[ub][:, c0:c0 + 128], Vp[buf][:, kblk, h, :], pt[:, s * 128:(s + 1) * 128],
                                            start=(s % 2 == 0), stop=(s % 2 == 1))
                        return inst
                    vres = sorted({("Vp", buf, slots[s][0] // 8, h) for s in range(4)})
                    S.add("pe", fn, reads=vres + [("PT", h, j % 2)], banks=[ub])

                def uevac(m, h):
                    ub = UB[h]
                    L = TOK // d
                    if d == 1:
                        o_ap = Uacc[:, h, 512 * m:512 * (m + 1)]
                        i_ap = ps[ub][:, :]
                    elif L >= 512:
                        r = (512 * m) // L
                        l0 = (512 * m) % L
                        o_ap = Uacc[:, h, :].rearrange("p (l r) -> p r l", r=d)[:, r, l0:l0 + 512]
                        i_ap = ps[ub][:, :]
                    else:
                        r0 = (512 * m) // L
                        nr = 512 // L
                        o_ap = Uacc[:, h, :].rearrange("p (l r) -> p r l", r=d)[:, r0:r0 + nr, :]
                        i_ap = ps[ub][:, :].rearrange("p (r l) -> p r l", r=nr)
                    allu = [("Uacc", h, mm) for mm in range(4)]
                    if ui == 0:
                        S.add("dve", lambda e: e.tensor_copy(out=o_ap, in_=i_ap),
                              writes=allu, banks=[ub])
                    else:
                        S.add("dve", lambda e: e.tensor_tensor(out=o_ap, in0=o_ap, in1=i_ap, op=ALU.add),
                              reads=allu, writes=[("Uacc", h, m)], banks=[ub])

                def finalize_piece(m):
                    sl = slice(512 * m, 512 * (m + 1))
                    rzm = rz[:, m % 2, :]
                    zs = ((Uacc[64:128, 0, sl], rzm[0:64, :], 0), (Uacc[0:64, 1, sl], rzm[64:128, :], 1))
                    for (z, ro, h) in zs:
                        S.add("act", lambda e, z=z: e.activation(out=z, in_=z, func=AF.Ln),
                              reads=[("Uacc", h, m)], writes=[("Uacc", h, m)])
                        S.add("act", lambda e, z=z, ro=ro: e.activation(out=ro, in_=z, func=AF.Exp, scale=-1.0),
                              reads=[("Uacc", h, m)], writes=[("rz", h, m % 2)])
                    S.add("dve", lambda e: e.tensor_tensor(out=rzm, in0=rzm, in1=gatt[:, sl], op=ALU.mult),
                          reads=[("rz", 0, m % 2), ("rz", 1, m % 2), ("gatt", m)], writes=[("rz", 0, m % 2), ("rz", 1, m % 2)])
                    S.add("dve", lambda e: e.tensor_tensor(out=aT[0:64, hp, sl], in0=Uacc[0:64, 0, sl], in1=rzm[0:64, :], op=ALU.mult),
                          reads=[("Uacc", 0, m), ("rz", 0, m % 2)], writes=[("aT", hp, 0, m)])
                    S.add("dve", lambda e: e.tensor_tensor(out=aT[64:128, hp, sl], in0=Uacc[64:128, 1, sl], in1=rzm[64:128, :], op=ALU.mult),
                          reads=[("Uacc", 1, m), ("rz", 1, m % 2)], writes=[("aT", hp, 1, m)])

                jobs = proj_jobs(u + 1) if u + 1 < 12 else []
                cuts = [(len(jobs) * j) // 8 for j in range(9)]
                pending = list(deferred)
                del deferred[:]
                for h in range(2):
                    qk(0, h)
                for j in range(8):
                    for h in range(2):
                        softmax_part(j, h)
                    if ui == 2 and j >= 2 and j % 2 == 0:
                        finalize_piece(j // 2 - 1)
                    if j + 1 < 8:
                        for h in range(2):
                            qk(j + 1, h)
                    if j == 0 and pending:
                        for fz in pending:
                            fz()
                    for job in jobs[cuts[j]:cuts[j + 1]]:
                        job()
                    for h in range(2):
                        pv(j, h)
                    if j % 2 == 1:
                        for h in range(2):
                            uevac(j // 2, h)
                        if ui == 2 and j == 7:
                            deferred.append(lambda: finalize_piece(3))

            deferred = []
            for job in proj_jobs(0):
                job()
            for u in range(12):
                step(u)
            for fz in deferred:
                fz()
            if debug:
                S.add("sp", lambda e: [e.dma_start(out=dbg_a[:, :], in_=aT[:].rearrange("p a t -> p (a t)"))],
                      reads=[("aT", hp, h, m) for hp in range(4) for h in range(2) for m in range(4)], dma="dbg")
            S.emit()

        es23 = ExitStack()
        uT = es23.enter_context(nc.sbuf_tensor("uT", [128, 8, TOK], BF16))
        bank_rot = [0]

        def next_bank():
            b = bank_rot[0]
            bank_rot[0] = (b + 1) % 8
            return b

        with ExitStack() as p2:
            def sb2(name, shape, dt):
                return p2.enter_context(nc.sbuf_tensor(name, shape, dt))
            hS = sb2("hS", [128, TOK + 2], F32)
            chS = sb2("chS", [128, TOK + 2], F32)
            acc = sb2("acc", [128, TOK], F32)
            sg = sb2("sg", [128, TOK], F32)
            ttiles = [(TOK - 2, 2, 0)] + [(TOK + 512 * w, 512, 2 + 512 * w) for w in range(4)]
            allch = [("chS", d0) for (_, _, d0) in ttiles]
            allacc = [("acc", w) for w in range(4)]
            for cc in range(8):
                u = 12 + cc
                load_unit(u + 2)
                W = wv(u, 512)
                wres = ("wr", u % 3)
                for (c0, n, d0) in ttiles:
                    b = next_bank()
                    proj(b, lambda kc: W[:, kc, 0:128], lambda kc, c0=c0, n=n: xT[:, kc, c0:c0 + n], n,
                         reads=[wres] + xres(c0, n))
                    S.add("act", lambda e, b=b, n=n, d0=d0: e.activation(out=hS[:, d0:d0 + n], in_=ps[b][:, 0:n], func=AF.Copy),
                          writes=[("hS", d0)], banks=[b])
                for (c0, n, d0) in ttiles:
                    b = next_bank()
                    proj(b, lambda kc: W[:, kc, 256:384], lambda kc, c0=c0, n=n: xT[:, kc, c0:c0 + n], n,
                         reads=[wres] + xres(c0, n))
                    S.add("dve", lambda e, b=b, n=n, d0=d0: e.tensor_tensor(out=chS[:, d0:d0 + n], in0=ps[b][:, 0:n],
                                                                            in1=hS[:, d0:d0 + n], op=ALU.mult),
                          reads=[("hS", d0)], writes=[("chS", d0)], banks=[b])
                w0 = cst[:, cc * 3 + 0:cc * 3 + 1]
                w1 = cst[:, cc * 3 + 1:cc * 3 + 2]
                w2 = cst[:, cc * 3 + 2:cc * 3 + 3]
                S.add("act", lambda e, w2=w2: e.activation(out=acc[:, :], in_=chS[:, 2:TOK + 2], func=AF.Copy, scale=w2),
                      reads=allch + ["cst"], writes=allacc)
                S.add("dve", lambda e, w1=w1: e.scalar_tensor_tensor(out=acc[:, :], in0=chS[:, 1:TOK + 1], scalar=w1,
                                                                     in1=acc[:, :], op0=ALU.mult, op1=ALU.add),
                      reads=allch + allacc + ["cst"], writes=allacc)
                S.add("dve", lambda e, w0=w0: e.scalar_tensor_tensor(out=acc[:, :], in0=chS[:, 0:TOK], scalar=w0,
                                                                     in1=acc[:, :], op0=ALU.mult, op1=ALU.add),
                      reads=allch + allacc + ["cst"], writes=allacc)
                for w in range(4):
                    b = next_bank()
                    proj(b, lambda kc: W[:, kc, 384:512], lambda kc, w=w: xT[:, kc, TOK + 512 * w:TOK + 512 * (w + 1)], 512,
                         reads=[wres, ("xT", 4 + w)])
                    S.add("act", lambda e, b=b, w=w: e.activation(out=sg[:, 512 * w:512 * (w + 1)], in_=ps[b][:, :], func=AF.Silu),
                          writes=[("sg", w)], banks=[b])
                for w in range(4):
                    b = next_bank()
                    sl = slice(512 * w, 512 * (w + 1))
                    proj(b, lambda kc: W[:, kc, 128:256], lambda kc, w=w: xT[:, kc, TOK + 512 * w:TOK + 512 * (w + 1)], 512,
                         reads=[wres, ("xT", 4 + w)])
                    S.add("dve", lambda e, b=b, sl=sl: e.tensor_tensor(out=acc[:, sl], in0=ps[b][:, :], in1=acc[:, sl], op=ALU.mult),
                          reads=[("acc", w)], writes=[("acc", w)], banks=[b])
                    S.add("dve", lambda e, sl=sl, cc=cc: e.tensor_tensor(out=uT[:, cc, sl], in0=acc[:, sl], in1=sg[:, sl], op=ALU.mult),
                          reads=[("acc", w), ("sg", w)], writes=[("uT", cc, w)])
            if debug:
                S.add("sp", lambda e: [e.dma_start(out=dbg_u[:, :], in_=uT[:].rearrange("p a t -> p (a t)"))],
                      reads=[("uT", cc, w) for cc in range(8) for w in range(4)], dma="dbg")
            S.emit()

        with ExitStack() as p3:
            def sb3(name, shape, dt):
                return p3.enter_context(nc.sbuf_tensor(name, shape, dt))
            gcS = [sb3(f"gcS{i}", [128, 512], F32) for i in range(2)]
            gaS = [sb3(f"gaS{i}", [128, 512], F32) for i in range(2)]
            mS = [sb3(f"mS{i}", [128, 512], F32) for i in range(2)]
            it = 0
            for j in range(8):
                u = 20 + j
                load_unit(u + 2)
                Wg = wr[u % 3][:, 0:2048].rearrange("p (k c) -> p k c", c=256)
                Wco = wr[u % 3][:, 2048:3072].rearrange("p (k c) -> p k c", c=128)
                Wao = wr[u % 3][:, 3072:3584].rearrange("p (k c) -> p k c", c=128)
                wres = ("wr", u % 3)
                for w in range(4):
                    i2 = it % 2
                    it += 1
                    tsl = slice(512 * w, 512 * (w + 1))
                    xsl = slice(TOK + 512 * w, TOK + 512 * (w + 1))
                    b = next_bank()
                    proj(b, lambda kc: Wg[:, kc, 0:128], lambda kc, xsl=xsl: xT[:, kc, xsl], 512, reads=[wres, ("xT", 4 + w)])
                    S.add("act", lambda e, b=b, i2=i2, j=j: e.activation(out=gcS[i2][:, :], in_=ps[b][:, :], func=AF.Sigmoid,
                                                                         bias=cst[:, 24 + j:25 + j]),
                          reads=["cst"], writes=[("gcS", i2)], banks=[b])
                    b = next_bank()
                    proj(b, lambda kc: Wg[:, kc, 128:256], lambda kc, xsl=xsl: xT[:, kc, xsl], 512, reads=[wres, ("xT", 4 + w)])
                    S.add("act", lambda e, b=b, i2=i2, j=j: e.activation(out=gaS[i2][:, :], in_=ps[b][:, :], func=AF.Sigmoid,
                                                                         bias=cst[:, 32 + j:33 + j]),
                          reads=["cst"], writes=[("gaS", i2)], banks=[b])
                    b = next_bank()
                    proj(b, lambda kc: Wco[:, kc, :], lambda kc, tsl=tsl: uT[:, kc, tsl], 512,
                         reads=[wres] + [("uT", cc, w) for cc in range(8)])
                    S.add("dve", lambda e, b=b, i2=i2: e.tensor_tensor(out=mS[i2][:, :], in0=ps[b][:, :], in1=gcS[i2][:, :], op=ALU.mult),
                          reads=[("gcS", i2)], writes=[("mS", i2)], banks=[b])
                    b = next_bank()
                    proj(b, lambda kc: Wao[:, kc, :], lambda kc, tsl=tsl: aT[:, kc, tsl], 512, nk=4,
                         reads=[wres] + [("aT", hp, h, w) for hp in range(4) for h in range(2)])
                    S.add("dve", lambda e, b=b, i2=i2: e.tensor_tensor(out=gaS[i2][:, :], in0=ps[b][:, :], in1=gaS[i2][:, :], op=ALU.mult),
                          reads=[("gaS", i2)], writes=[("gaS", i2)], banks=[b])
                    S.add("dve", lambda e, i2=i2, j=j, tsl=tsl: e.tensor_tensor(out=mT[:, j, tsl], in0=mS[i2][:, :], in1=gaS[i2][:, :], op=ALU.add),
                          reads=[("mS", i2), ("gaS", i2)], writes=[("mT", j, w)])
            if debug:
                S.add("sp", lambda e: [e.dma_start(out=dbg_m[:, :], in_=low[:, :])],
                      reads=[("mT", j, w) for j in range(8) for w in range(4)], dma="dbg")
            S.emit()

        es23.close()
        es123.close()
        with ExitStack() as p4:
            def sb4(name, shape, dt):
                return p4.enter_context(nc.sbuf_tensor(name, shape, dt))
            gb = sb4("gb_sb", [128, 2 * DM], F32)
            NY = 4
            xt = [sb4(f"xt{i}", [128, DM], F32) for i in range(3)]
            ys = [sb4(f"ys{i}", [128, DM], F32) for i in range(NY)]
            ob = [sb4(f"ob{i}", [128, DM], F32) for i in range(3)]
            sa = [sb4(f"sa{i}", [128, 64], F32) for i in range(6)]
            junk = sb4("junk", [128, DM], BF16)
            sd = [sb4(f"sd{i}", [128, 64], F32) for i in range(6)]
            wo = [wv(28 + hf, 512) for hf in range(2)]
            S.add("sp", lambda e: [e.dma_start(out=gb[:], in_=gb_d[:, :])], writes=["gb"], dma="gb")
            S.add("dve", lambda e: e.tensor_copy(out=pp[3][:, :], in_=gb[:, 0:DM]),
                  reads=["gb"], writes=["gbp"], banks=[6, 7])
            out_toks = []

            def load_xt(t):
                S.add("sp", lambda e: [e.dma_start(out=xt[t % 3][:], in_=xtok_d[128 * t:128 * (t + 1), :])],
                      writes=[("xt", t % 3)], dma=f"xt{t % 3}")

            load_xt(0)
            load_xt(1)

            def stage_a1(t):
                i3, iy, i6, pb = t % 3, t % NY, t % 6, t % 3
                if t + 2 < 16:
                    load_xt(t + 2)
                for hf in range(2):
                    proj(2 * pb + hf, lambda kc: mT[:, kc, 128 * t:128 * (t + 1)], lambda kc, hf=hf: wo[hf][:, kc, :], 512,
                         reads=[("mT", j, t // 4) for j in range(8)] + [("wr", (28 + hf) % 3)])
                S.add("dve", lambda e: e.scalar_tensor_tensor(
                    out=ys[iy][:, :], in0=xt[i3][:, :], scalar=ALPHA, in1=pp[pb][:, :], op0=ALU.mult, op1=ALU.add),
                    reads=[("xt", i3)], writes=[("ys", iy)], banks=[2 * pb, 2 * pb + 1])
                S.add("act", lambda e: e.activation(out=junk[:, :], in_=ys[iy][:, :], func=AF.Copy, scale=1.0 / DM,
                                                    accum_out=sa[i6][:, 0:1]),
                      reads=[("ys", iy)], writes=["junk", ("sa", i6, 0)])
                S.add("act", lambda e: e.activation(out=junk[:, :], in_=ys[iy][:, :], func=AF.Square, scale=DM ** -0.5,
                                                    accum_out=sa[i6][:, 1:2]),
                      reads=[("ys", iy)], writes=["junk", ("sa", i6, 1)])

            def stage_a2(t):
                i6 = t % 6
                S.add("dve", lambda e: e.tensor_tensor(out=sd[i6][:, 0:1], in0=sa[i6][:, 0:1], in1=sa[i6][:, 0:1], op=ALU.mult),
                      reads=[("sa", i6, 0)], writes=[("sd", i6, 0)])
                S.add("dve", lambda e: e.tensor_scalar(out=sd[i6][:, 4:5], in0=sa[i6][:, 0:1], scalar1=-1.0, scalar2=None, op0=ALU.mult),
                      reads=[("sa", i6, 0)], writes=[("sd", i6, 4)])
                S.add("dve", lambda e: e.tensor_tensor(out=sd[i6][:, 1:2], in0=sa[i6][:, 1:2], in1=sd[i6][:, 0:1], op=ALU.subtract),
                      reads=[("sa", i6, 1), ("sd", i6, 0)], writes=[("sd", i6, 1)])
                S.add("act", lambda e: e.activation(out=sa[i6][:, 2:3], in_=sd[i6][:, 1:2], func=AF.Sqrt, bias=LN_EPS),
                      reads=[("sd", i6, 1)], writes=[("sa", i6, 2)])

            def stage_b(t):
                i3, iy, i6 = t % 3, t % NY, t % 6
                S.add("dve", lambda e: e.reciprocal(out=sd[i6][:, 2:3], in_=sa[i6][:, 2:3]),
                      reads=[("sa", i6, 2)], writes=[("sd", i6, 2)])
                S.add("dve", lambda e: e.tensor_tensor(out=sd[i6][:, 3:4], in0=sd[i6][:, 4:5], in1=sd[i6][:, 2:3], op=ALU.mult),
                      reads=[("sd", i6, 4), ("sd", i6, 2)], writes=[("sd", i6, 3)])
                S.add("act", lambda e: e.activation(out=ob[i3][:, :], in_=ys[iy][:, :], func=AF.Identity,
                                                    scale=sd[i6][:, 2:3], bias=sd[i6][:, 3:4]),
                      reads=[("ys", iy), ("sd", i6, 2), ("sd", i6, 3)], writes=[("ob", i3)])

            def stage_d(t):
                i3 = t % 3
                S.add("dve", lambda e: e.tensor_tensor(out=ob[i3][:, :], in0=ob[i3][:, :], in1=pp[3][:, :], op=ALU.mult),
                      reads=[("ob", i3), "gbp"], writes=[("ob", i3)], banks=[6, 7])
                S.add("pool", lambda e: e.tensor_tensor(out=ob[i3][:, :], in0=ob[i3][:, :], in1=gb[:, DM:2 * DM], op=ALU.add),
                      reads=[("ob", i3), "gb"], writes=[("ob", i3)])
                tok = S.add("pool", lambda e: [e.dma_start(out=out_d[128 * t:128 * (t + 1), :], in_=ob[i3][:])],
                            reads=[("ob", i3)], dma=f"ob{i3}")
                out_toks.append(tok)

            for t in range(16 + 3):
                if t < 16:
                    stage_a1(t)
                if 0 <= t - 2 < 16:
                    stage_b(t - 2)
                if 0 <= t - 1 < 16:
                    stage_a2(t - 1)
                if 0 <= t - 3 < 16:
                    stage_d(t - 3)
            finals = out_toks[-3:]
            if debug:
                finals = finals + [("d:dbg", S.dma_cnt["d:dbg"], "sp", True)]
            S.emit(final_waits=finals)
    return nc


_NC_CACHE = {}


def _get_nc(debug=False):
    if debug not in _NC_CACHE:
        _NC_CACHE[debug] = build(debug)
    return _NC_CACHE[debug]


def make_in_maps(x, w_in, conv_w, w_conv_out, w_att_out, b_gate, w_o, ln_g, ln_b):
    x2 = np.asarray(x, np.float32).reshape(SEQ, DM)
    ws, cst, gb = _prep_weights(w_in, conv_w, w_conv_out, w_att_out, b_gate, w_o, ln_g, ln_b)
    in_maps = []
    for c in range(NCORES):
        own = x2[c * TOK:(c + 1) * TOK]
        halo = x2[(c - 1) * TOK:c * TOK] if c > 0 else np.zeros((TOK, DM), np.float32)
        xe = np.concatenate([halo, own], axis=0)
        xT = np.ascontiguousarray(xe.T).reshape(8, 128, XC)
        in_maps.append({"xT": xT, "xtok": np.ascontiguousarray(own), "ws": ws,
                        "cst": cst, "gb": gb, "mk": _masks(c)})
    return in_maps


def kernel(x, w_in, conv_w, w_conv_out, w_att_out, b_gate, w_o, ln_g, ln_b):
    nc = _get_nc(False)
    in_maps = make_in_maps(x, w_in, conv_w, w_conv_out, w_att_out, b_gate, w_o, ln_g, ln_b)
    res = run_bass_kernel_spmd(nc, in_maps, core_ids=list(range(NCORES)))
    out = np.concatenate([r["out"] for r in res.results], axis=0)
    return out.reshape(1, SEQ, DM).astype(np.float32)
```

```python
import numpy as np
import concourse.bass as bass
import concourse.mybir as mybir
from concourse.bass_utils import run_bass_kernel_spmd

F32 = mybir.dt.float32
BF16 = mybir.dt.bfloat16
AF = mybir.ActivationFunctionType
ALU = mybir.AluOpType

NCORES = 8
SEQ = 16384
DM = 1024
TOK = SEQ // NCORES
XC = 2 * TOK
DILS = (1, 4, 16)
GORD = (1, 2, 0)
ALPHA = 2.0 ** 0.25
LN_EPS = 1e-5
NUNITS = 30
UW = 4096

Q0, K0, V0, GA0, H0, B0, C0, GC0, GL0 = 0, 1536, 3072, 4608, 5120, 6144, 7168, 8192, 9216


class Sched:
    ENGS = ("pe", "act", "dve", "pool", "sp")

    def __init__(self, nc, esems, dma_sems):
        self.nc = nc
        self.semobj = {"e:" + e: s for e, s in esems.items()}
        self.free_dma = list(dma_sems)
        self.ops = {e: [] for e in self.ENGS}
        self.count = {e: 0 for e in self.ENGS}
        self.lastw = {}
        self.readers = {}
        self.bank_last = {}
        self.waited = {e: {} for e in self.ENGS}
        self.dma_cnt = {}
        self.n_ops = 0

    def _slot(self, slot):
        key = "d:" + slot
        if key not in self.semobj:
            self.semobj[key] = self.free_dma.pop()
            self.dma_cnt[key] = 0
        return key

    def add(self, eng, fn, reads=(), writes=(), banks=(), dma=None, ndma=1):
        is_dma = dma is not None
        deps = {}

        def need(tok, kind):
            if tok is None:
                return
            semkey, val, teng, tdma = tok
            if not tdma and teng == eng and not is_dma and eng == "pe":
                return
            if deps.get(semkey, 0) < val:
                deps[semkey] = val

        for r in reads:
            need(self.lastw.get(r), "raw")
        for w in writes:
            need(self.lastw.get(w), "waw")
            for t in self.readers.get(w, ()):
                need(t, "war")
        for b in banks:
            t = self.bank_last.get(b)
            if t is not None and t[2] != eng:
                need(t, "bank")

        if is_dma:
            key = self._slot(dma)
            self.dma_cnt[key] += 16 * ndma
            tok = (key, self.dma_cnt[key], eng, True)
        else:
            self.count[eng] += 1
            tok = ("e:" + eng, self.count[eng], eng, False)
        for r in reads:
            self.readers.setdefault(r, []).append(tok)
        for w in writes:
            self.lastw[w] = tok
            self.readers[w] = []
        for b in banks:
            self.bank_last[b] = tok
        self.ops[eng].append((fn, deps, tok))
        self.n_ops += 1
        return tok

    def emit(self, final_waits=()):
        nc = self.nc
        with nc.Block() as block:
            deco = {"pe": block.tensor, "act": block.scalar, "dve": block.vector,
                    "pool": block.gpsimd, "sp": block.sync}
            for eng in self.ENGS:
                oplist = self.ops[eng]
                fw = list(final_waits) if eng == "sp" else []
                if not oplist and not fw:
                    continue

                def body(e, oplist=oplist, eng=eng, fw=fw):
                    waited = self.waited[eng]
                    for fn, deps, tok in oplist:
                        for semkey, val in deps.items():
                            if waited.get(semkey, 0) >= val:
                                continue
                            e.wait_ge(self.semobj[semkey], val)
                            waited[semkey] = val
                        res = fn(e)
                        if tok[3]:
                            for inst in res:
                                inst.then_inc(self.semobj[tok[0]], 16)
                        else:
                            res.then_inc(self.semobj[tok[0]], 1)
                    for semkey, val, _, _ in fw:
                        if waited.get(semkey, 0) >= val:
                            continue
                        e.wait_ge(self.semobj[semkey], val)
                        waited[semkey] = val

                deco[eng](body)
        self.ops = {e: [] for e in self.ENGS}


def _unit(wc):
    k, n = wc.shape
    nk = k // 128
    return np.ascontiguousarray(wc.reshape(nk, 128, n).transpose(1, 0, 2)).reshape(128, nk * n)


def _prep_weights(w_in, conv_w, w_conv_out, w_att_out, b_gate, w_o, ln_g, ln_b):
    W = np.asarray(w_in[0], np.float32)
    wco = np.asarray(w_conv_out[0], np.float32)
    wao = np.asarray(w_att_out[0], np.float32)
    ws = np.zeros((NUNITS, 128, UW), np.float32)
    zeros128 = np.zeros((DM, 128), np.float32)
    for hp in range(4):
        for i, g in enumerate(GORD):
            c = g * 512 + hp * 128
            cols = [W[:, Q0 + c:Q0 + c + 128], W[:, K0 + c:K0 + c + 128], W[:, V0 + c:V0 + c + 128],
                    W[:, GA0 + hp * 128:GA0 + hp * 128 + 128] if i == 1 else zeros128]
            ws[hp * 3 + i] = _unit(np.concatenate(cols, axis=1))
    for cc in range(8):
        c = cc * 128
        cols = [W[:, H0 + c:H0 + c + 128], W[:, B0 + c:B0 + c + 128],
                W[:, C0 + c:C0 + c + 128], W[:, GC0 + c:GC0 + c + 128]]
        ws[12 + cc] = _unit(np.concatenate(cols, axis=1))
    for j in range(8):
        c = j * 128
        gw = np.concatenate([W[:, GL0 + c:GL0 + c + 128], W[:, GL0 + 1024 + c:GL0 + 1024 + c + 128]], axis=1)
        ws[20 + j, :, 0:2048] = _unit(gw)
        ws[20 + j, :, 2048:3072] = _unit(wco[:, c:c + 128])
        ws[20 + j, :, 3072:3584] = _unit(wao[:, c:c + 128])
    wof = np.asarray(w_o[0], np.float32)
    ws[28] = _unit(wof[:, 0:512])
    ws[29] = _unit(wof[:, 512:1024])
    cst = np.zeros((128, 40), np.float32)
    cw = np.asarray(conv_w[0], np.float32)
    cst[:, 0:24] = cw.reshape(3, 8, 128).transpose(2, 1, 0).reshape(128, 24)
    cst[:, 24:40] = np.asarray(b_gate[0], np.float32).reshape(16, 128).T
    gb = np.concatenate([np.broadcast_to(np.asarray(ln_g[0], np.float32), (128, DM)),
                         np.broadcast_to(np.asarray(ln_b[0], np.float32), (128, DM))], axis=1)
    return ws, cst, np.ascontiguousarray(gb)


def _masks(core):
    j = np.arange(128)[:, None]
    i = np.arange(128)[None, :]
    p = (j >= i).astype(np.float32)
    c = (j <= i).astype(np.float32)
    ph = p * (1.0 if core > 0 else 0.0)
    ident = np.eye(128, dtype=np.float32)
    return np.ascontiguousarray(np.concatenate([ph, c, p, c, p, c, p, c, ph, c, ph, c, ident], axis=1))


MKW = 1536 + 128


def build(debug=False):
    from contextlib import ExitStack
    nc = bass.Bass("TRN2", target_bir_lowering=False)
    xT_d = nc.dram_tensor("xT", [8, 128, XC], F32, kind="ExternalInput").ap()
    xtok_d = nc.dram_tensor("xtok", [TOK, DM], F32, kind="ExternalInput").ap()
    ws_d = nc.dram_tensor("ws", [NUNITS, 128, UW], F32, kind="ExternalInput").ap()
    cst_d = nc.dram_tensor("cst", [128, 40], F32, kind="ExternalInput").ap()
    gb_d = nc.dram_tensor("gb", [128, 2 * DM], F32, kind="ExternalInput").ap()
    mk_d = nc.dram_tensor("mk", [128, MKW], F32, kind="ExternalInput").ap()
    out_d = nc.dram_tensor("out", [TOK, DM], F32, kind="ExternalOutput").ap()
    if debug:
        dbg_a = nc.dram_tensor("dbg_a", [128, 4 * TOK], BF16, kind="ExternalOutput").ap()
        dbg_u = nc.dram_tensor("dbg_u", [128, 8 * TOK], BF16, kind="ExternalOutput").ap()
        dbg_m = nc.dram_tensor("dbg_m", [128, 8 * TOK], BF16, kind="ExternalOutput").ap()

    with ExitStack() as es:
        def sb(name, shape, dt):
            return es.enter_context(nc.sbuf_tensor(name, shape, dt))

        mk = sb("mk_bf", [128, MKW], BF16)
        ident = mk[:, 1536:1664]
        cst = sb("cst_sb", [128, 40], F32)
        wr = [sb(f"wr{i}", [128, UW], BF16) for i in range(3)]
        low = sb("low32", [128, 16384], BF16)
        mT = low[:, :].rearrange("p (k t) -> p k t", k=8)
        es123 = ExitStack()
        xT = es123.enter_context(nc.sbuf_tensor("xT_bf", [128, 8, XC], BF16))
        aT = es123.enter_context(nc.sbuf_tensor("aT", [128, 4, TOK], BF16))
        pp = [es.enter_context(nc.psum_tensor(f"pp{i}", [128, 1024], F32)) for i in range(4)]
        ps = [pp[i // 2][:, (i % 2) * 512:(i % 2 + 1) * 512] for i in range(8)]
        esems = {e: es.enter_context(nc.semaphore("sem_" + e)) for e in Sched.ENGS}
        dsems = [es.enter_context(nc.semaphore(f"dsem{i}")) for i in range(32)]
        S = Sched(nc, esems, dsems)

        def load_unit(u):
            if u >= NUNITS:
                return
            buf = wr[u % 3]
            if u < 12 and u % 3 != 1:
                o_ap = buf[:, :].rearrange("p (k c) -> p k c", c=512)[:, :, 0:384]
                i_ap = ws_d[u].rearrange("p (k c) -> p k c", c=512)[:, :, 0:384]
            else:
                o_ap, i_ap = buf[:], ws_d[u]
            S.add("pool", lambda e: [e.dma_start(out=o_ap, in_=i_ap)],
                  writes=[("wr", u % 3)], dma=f"wr{u % 3}")

        def wv(u, width):
            return wr[u % 3][:, 0:8 * width].rearrange("p (k c) -> p k c", c=width)

        load_unit(0)
        xsrc = xT_d.rearrange("k p c -> p k c")
        xpieces = [(3, 1536, 2048), (4, 2048, 2560), (5, 2560, 3072), (6, 3072, 3584), (7, 3584, 4096),
                   (2, 1024, 1536), (1, 512, 1024), (0, 0, 512)]
        for (nm, c0, c1) in xpieces:
            S.add("pool", lambda e, c0=c0, c1=c1: [e.dma_start(out=xT[:, :, c0:c1], in_=xsrc[:, :, c0:c1])],
                  writes=[("xT", nm)], dma=f"xT{nm}")
            if nm == 4:
                S.add("pool", lambda e: [e.dma_start(out=mk[:], in_=mk_d[:, :])], writes=["mk"], dma="mk")
            if nm == 7:
                load_unit(1)
        S.add("sp", lambda e: [e.dma_start(out=cst[:], in_=cst_d[:, :])], writes=["cst"], dma="cst")

        def xres(c0, n):
            out = []
            for (nm, a, b) in xpieces:
                if a < c0 + n and c0 < b:
                    out.append(("xT", nm))
            return out

        def proj(bank, lhs_fn, rhs_fn, n, nk=8, reads=()):
            lhs = [lhs_fn(kc) for kc in range(nk)]
            rhs = [rhs_fn(kc) for kc in range(nk)]

            def fn(e):
                inst = None
                for kc in range(nk):
                    inst = e.matmul(ps[bank][:, 0:n], lhs[kc], rhs[kc],
                                    start=(kc == 0), stop=(kc == nk - 1))
                return inst
            S.add("pe", fn, reads=list(reads), banks=[bank])

        with ExitStack() as p1:
            def sb1(name, shape, dt):
                return p1.enter_context(nc.sbuf_tensor(name, shape, dt))
            Vp1t = sb1("Vp1", [128, 8192], BF16)
            Vp = [low[:, 0:8192].rearrange("p (b h c) -> p b h c", h=2, c=128),
                  Vp1t[:, :].rearrange("p (b h c) -> p b h c", h=2, c=128)]
            PT = [[low[:, 8192 + (2 * h + i) * 512:8192 + (2 * h + i + 1) * 512] for i in range(2)] for h in range(2)]
            QT = [low[:, 10240:12288], sb1("QT1", [128, TOK], BF16)]
            KT = [low[:, 12288:16384], sb1("KT1", [128, 2 * TOK], BF16)]
            VT = sb1("VT0", [128, XC], BF16)
            Uacc = sb1("Uacc", [128, 2, TOK], F32)
            gatt = sb1("gatt", [128, TOK], F32)
            rz = sb1("rz", [128, 2, 512], F32)

            S.add("dve", lambda e: e.memset(low[:, 0:8192], 1.0),
                  writes=[("Vp", 0, gi, h) for gi in range(4) for h in range(2)])
            S.add("dve", lambda e: e.memset(Vp1t[:, :], 1.0),
                  writes=[("Vp", 1, gi, h) for gi in range(4) for h in range(2)])

            pj_rot = [0]

            def next_pj():
                b = pj_rot[0]
                pj_rot[0] = (b + 1) % 2
                return b

            SB = {0: [2, 3], 1: [4, 5]}
            UB = {0: 6, 1: 7}

            def ktiles(d):
                tiles = []
                t0 = -128 * d
                while t0 < 0:
                    n = min(512, -t0)
                    tiles.append((t0, n))
                    t0 += n
                return tiles + [(512 * w, 512) for w in range(4)]

            def proj_jobs(u):
                hp, ui = divmod(u, 3)
                g = GORD[ui]
                d = DILS[g]
                nb = 16 // d
                buf = u % 2
                W = wv(u, 512)
                wres = ("wr", u % 3)
                jobs = []
                tiles = ktiles(d)
                klen = (nb + 1) * 128

                def vt_job(ti, t0, n):
                    b = next_pj()
                    proj(b, lambda kc: W[:, kc, 256:384], lambda kc: xT[:, kc, TOK + t0:TOK + t0 + n], n,
                         reads=[wres] + xres(TOK + t0, n))
                    S.add("act", lambda e: e.activation(out=VT[:, TOK + t0:TOK + t0 + n], in_=ps[b][:, 0:n], func=AF.Copy),
                          writes=[("VT", ti)], banks=[b])

                def k_job(ti, t0, n):
                    b = next_pj()
                    proj(b, lambda kc: W[:, kc, 128:256], lambda kc: xT[:, kc, TOK + t0:TOK + t0 + n], n,
                         reads=[wres] + xres(TOK + t0, n))
                    lk0 = (t0 + 128 * d) // d
                    cnt = n // d
                    if d == 1:
                        o_ap = KT[buf][:, lk0:lk0 + cnt]
                        i_ap = ps[b][:, 0:n]
                    else:
                        o_ap = KT[buf][:, 0:d * klen].rearrange("p (r l) -> p r l", r=d)[:, :, lk0:lk0 + cnt]
                        i_ap = ps[b][:, 0:n].rearrange("p (l r) -> p r l", r=d)
                    S.add("act", lambda e: e.activation(out=o_ap, in_=i_ap, func=AF.Copy),
                          writes=[("KT", buf, ti)], banks=[b])

                def q_job(w):
                    b = next_pj()
                    proj(b, lambda kc: W[:, kc, 0:128], lambda kc: xT[:, kc, TOK + 512 * w:TOK + 512 * (w + 1)], 512,
                         reads=[wres, ("xT", 4 + w)])
                    lk0 = 512 * w // d
                    cnt = 512 // d
                    if d == 1:
                        o_ap = QT[buf][:, lk0:lk0 + cnt]
                        i_ap = ps[b][:, 0:512]
                    else:
                        o_ap = QT[buf][:, :].rearrange("p (r l) -> p r l", r=d)[:, :, lk0:lk0 + cnt]
                        i_ap = ps[b][:, 0:512].rearrange("p (l r) -> p r l", r=d)
                    S.add("act", lambda e: e.activation(out=o_ap, in_=i_ap, func=AF.Copy, scale=0.125),
                          writes=[("QT", buf, w)], banks=[b])

                def g_job(w):
                    b = next_pj()
                    proj(b, lambda kc: W[:, kc, 384:512], lambda kc: xT[:, kc, TOK + 512 * w:TOK + 512 * (w + 1)], 512,
                         reads=[wres, ("xT", 4 + w)])
                    S.add("act", lambda e: e.activation(out=gatt[:, 512 * w:512 * (w + 1)], in_=ps[b][:, 0:512], func=AF.Silu),
                          writes=[("gatt", w)], banks=[b])

                def t_job(b0):
                    nblk = d * (nb + 1)
                    nbk = min(8, nblk - b0)
                    b = next_pj()
                    pbf = ps[b][:, :].bitcast(BF16)

                    def fnt(e):
                        inst = None
                        for q in range(nbk):
                            vb = b0 + q
                            r, kb = vb // (nb + 1), vb % (nb + 1)
                            st = TOK - 128 * d + kb * 128 * d + r
                            inst = e.transpose(pbf[:, q * 128:(q + 1) * 128], VT[:, st:st + 127 * d + 1:d], ident)
                        return inst
                    S.add("pe", fnt, reads=["mk"] + [("VT", ti) for ti in range(len(tiles))], banks=[b])
                    for h in range(2):
                        o_ap = Vp[buf][:, b0:b0 + nbk, h, 64 * h:64 * h + 64]
                        i_ap = pbf[:, 0:nbk * 128].rearrange("p (q c) -> p q c", c=128)[:, :, 64 * h:64 * h + 64]
                        S.add("dve", lambda e, o=o_ap, i=i_ap: e.tensor_copy(out=o, in_=i),
                              writes=[("Vp", buf, b0 // 8, h)], banks=[b])

                tjobs = [(lambda b0=b0: t_job(b0)) for b0 in range(0, d * (nb + 1), 8)]
                if u == 0:
                    for ti, (t0, n) in enumerate(tiles):
                        jobs.append(lambda ti=ti, t0=t0, n=n: vt_job(ti, t0, n))
                        jobs.append(lambda ti=ti, t0=t0, n=n: k_job(ti, t0, n))
                        if t0 >= 0:
                            jobs.append(lambda w=t0 // 512: q_job(w))
                    jobs += tjobs
                elif u == 1:
                    own = [(ti, t0, n) for ti, (t0, n) in enumerate(tiles) if t0 >= 0]
                    halo = [(ti, t0, n) for ti, (t0, n) in enumerate(tiles) if t0 < 0]
                    for (ti, t0, n) in own:
                        jobs.append(lambda ti=ti, t0=t0, n=n: vt_job(ti, t0, n))
                    for (ti, t0, n) in own:
                        jobs.append(lambda ti=ti, t0=t0, n=n: k_job(ti, t0, n))
                    for w in range(4):
                        jobs.append(lambda w=w: q_job(w))
                    for w in range(4):
                        jobs.append(lambda w=w: g_job(w))
                    for (ti, t0, n) in reversed(halo):
                        jobs.append(lambda ti=ti, t0=t0, n=n: vt_job(ti, t0, n))
                    hk = [(lambda ti=ti, t0=t0, n=n: k_job(ti, t0, n)) for (ti, t0, n) in reversed(halo)]
                    for i in range(max(len(hk), len(tjobs))):
                        if i < len(hk):
                            jobs.append(hk[i])
                        if i < len(tjobs):
                            jobs.append(tjobs[i])
                    return jobs
                else:
                    for ti, (t0, n) in enumerate(tiles):
                        jobs.append(lambda ti=ti, t0=t0, n=n: vt_job(ti, t0, n))
                    for ti, (t0, n) in enumerate(tiles):
                        jobs.append(lambda ti=ti, t0=t0, n=n: k_job(ti, t0, n))
                    qj = [(lambda w=w: q_job(w)) for w in range(4)]
                    qi = 0
                    for tj in tjobs[:-1]:
                        jobs.append(qj[qi])
                        qi += 1
                        jobs.append(tj)
                    jobs += qj[qi:]
                    jobs.append(tjobs[-1])
                if ui == 1:
                    for w in range(4):
                        jobs.append(lambda w=w: g_job(w))
                return jobs

            def step(u):
                hp, ui = divmod(u, 3)
                g = GORD[ui]
                d = DILS[g]
                nb = 16 // d
                buf = u % 2
                ntile = len(ktiles(d))
                load_unit(u + 2)
                qres = [("QT", buf, w) for w in range(4)]
                kres = [("KT", buf, ti) for ti in range(ntile)]

                def sbank_desc(j):
                    slots = []
                    for q in (2 * j, 2 * j + 1):
                        r, qb = q // nb, q % nb
                        kprev = r * (nb + 1) + qb
                        slots.append((kprev, q))
                        slots.append((kprev + 1, q))
                    return slots

                def qk(j, h):
                    slots = sbank_desc(j)
                    bank = SB[h][j % 2]
                    groups = []
                    s = 0
                    while s < 4:
                        if s + 1 < 4 and slots[s + 1][0] == slots[s][0] and slots[s + 1][1] == slots[s][1] + 1:
                            groups.append((s, 2))
                            s += 2
                        else:
                            groups.append((s, 1))
                            s += 1

                    def fn(e):
                        inst = None
                        for (s0, cnt) in groups:
                            kblk, q = slots[s0]
                            inst = e.matmul(ps[bank][:, s0 * 128:(s0 + cnt) * 128],
                                            KT[buf][64 * h:64 * h + 64, kblk * 128:(kblk + 1) * 128],
                                            QT[buf][64 * h:64 * h + 64, q * 128:(q + cnt) * 128],
                                            start=True, stop=True)
                        return inst
                    S.add("pe", fn, reads=qres + kres, banks=[bank])

                def softmax_part(j, h):
                    bank = SB[h][j % 2]
                    pt = PT[h][j % 2]
                    if nb == 1:
                        mvar = 2
                    else:
                        mvar = 0 if (2 * j) % nb == 0 else 1
                    S.add("act", lambda e: e.activation(out=pt, in_=ps[bank][:, :], func=AF.Exp),
                          writes=[("PT", h, j % 2)], banks=[bank])
                    S.add("dve", lambda e: e.tensor_tensor(out=pt, in0=pt, in1=mk[:, mvar * 512:(mvar + 1) * 512],
                                                           op=ALU.mult),
                          reads=["mk", ("PT", h, j % 2)], writes=[("PT", h, j % 2)])

                def pv(j, h):
                    slots = sbank_desc(j)
                    pt = PT[h][j % 2]
                    ub = UB[h]

                    def fn(e):
                        inst = None
                        for s in range(4):
                            kblk, q = slots[s]
                            c0 = (q % 4) * 128
                            inst = e.matmul(ps[ub][:, c0:c0 + 128], Vp[buf][:, kblk, h, :], pt[:, s * 128:(s + 1) * 128],
                                            start=(s % 2 == 0), stop=(s % 2 == 1))
                        return inst
                    vres = sorted({("Vp", buf, slots[s][0] // 8, h) for s in range(4)})
                    S.add("pe", fn, reads=vres + [("PT", h, j % 2)], banks=[ub])

                def uevac(m, h):
                    ub = UB[h]
                    L = TOK // d
                    if d == 1:
                        o_ap = Uacc[:, h, 512 * m:512 * (m + 1)]
                        i_ap = ps[ub][:, :]
                    elif L >= 512:
                        r = (512 * m) // L
                        l0 = (512 * m) % L
                        o_ap = Uacc[:, h, :].rearrange("p (l r) -> p r l", r=d)[:, r, l0:l0 + 512]
                        i_ap = ps[ub][:, :]
                    else:
                        r0 = (512 * m) // L
                        nr = 512 // L
                        o_ap = Uacc[:, h, :].rearrange("p (l r) -> p r l", r=d)[:, r0:r0 + nr, :]
                        i_ap = ps[ub][:, :].rearrange("p (r l) -> p r l", r=nr)
                    allu = [("Uacc", h, mm) for mm in range(4)]
                    if ui == 0:
                        S.add("dve", lambda e: e.tensor_copy(out=o_ap, in_=i_ap),
                              writes=allu, banks=[ub])
                    else:
                        S.add("dve", lambda e: e.tensor_tensor(out=o_ap, in0=o_ap, in1=i_ap, op=ALU.add),
                              reads=allu, writes=[("Uacc", h, m)], banks=[ub])

                def finalize_piece(m):
                    sl = slice(512 * m, 512 * (m + 1))
                    rzm = rz[:, m % 2, :]
                    zs = ((Uacc[64:128, 0, sl], rzm[0:64, :], 0), (Uacc[0:64, 1, sl], rzm[64:128, :], 1))
                    for (z, ro, h) in zs:
                        S.add("act", lambda e, z=z: e.activation(out=z, in_=z, func=AF.Ln),
                              reads=[("Uacc", h, m)], writes=[("Uacc", h, m)])
                        S.add("act", lambda e, z=z, ro=ro: e.activation(out=ro, in_=z, func=AF.Exp, scale=-1.0),
                              reads=[("Uacc", h, m)], writes=[("rz", h, m % 2)])
                    S.add("dve", lambda e: e.tensor_tensor(out=rzm, in0=rzm, in1=gatt[:, sl], op=ALU.mult),
                          reads=[("rz", 0, m % 2), ("rz", 1, m % 2), ("gatt", m)], writes=[("rz", 0, m % 2), ("rz", 1, m % 2)])
                    S.add("dve", lambda e: e.tensor_tensor(out=aT[0:64, hp, sl], in0=Uacc[0:64, 0, sl], in1=rzm[0:64, :], op=ALU.mult),
                          reads=[("Uacc", 0, m), ("rz", 0, m % 2)], writes=[("aT", hp, 0, m)])
                    S.add("dve", lambda e: e.tensor_tensor(out=aT[64:128, hp, sl], in0=Uacc[64:128, 1, sl], in1=rzm[64:128, :], op=ALU.mult),
                          reads=[("Uacc", 1, m), ("rz", 1, m % 2)], writes=[("aT", hp, 1, m)])

                jobs = proj_jobs(u + 1) if u + 1 < 12 else []
                cuts = [(len(jobs) * j) // 8 for j in range(9)]
                pending = list(deferred)
                del deferred[:]
                for h in range(2):
                    qk(0, h)
                for j in range(8):
                    for h in range(2):
                        softmax_part(j, h)
                    if ui == 2 and j >= 2 and j % 2 == 0:
                        finalize_piece(j // 2 - 1)
                    if j + 1 < 8:
                        for h in range(2):
                            qk(j + 1, h)
                    if j == 0 and pending:
                        for fz in pending:
                            fz()
                    for job in jobs[cuts[j]:cuts[j + 1]]:
                        job()
                    for h in range(2):
                        pv(j, h)
                    if j % 2 == 1:
                        for h in range(2):
                            uevac(j // 2, h)
                        if ui == 2 and j == 7:
                            deferred.append(lambda: finalize_piece(3))

            deferred = []
            for job in proj_jobs(0):
                job()
            for u in range(12):
                step(u)
            for fz in deferred:
                fz()
            if debug:
                S.add("sp", lambda e: [e.dma_start(out=dbg_a[:, :], in_=aT[:].rearrange("p a t -> p (a t)"))],
                      reads=[("aT", hp, h, m) for hp in range(4) for h in range(2) for m in range(4)], dma="dbg")
            S.emit()

        es23 = ExitStack()
        uT = es23.enter_context(nc.sbuf_tensor("uT", [128, 8, TOK], BF16))
        bank_rot = [0]

        def next_bank():
            b = bank_rot[0]
            bank_rot[0] = (b + 1) % 8
            return b

        with ExitStack() as p2:
            def sb2(name, shape, dt):
                return p2.enter_context(nc.sbuf_tensor(name, shape, dt))
            hS = sb2("hS", [128, TOK + 2], F32)
            chS = sb2("chS", [128, TOK + 2], F32)
            acc = sb2("acc", [128, TOK], F32)
            sg = sb2("sg", [128, TOK], F32)
            ttiles = [(TOK - 2, 2, 0)] + [(TOK + 512 * w, 512, 2 + 512 * w) for w in range(4)]
            allch = [("chS", d0) for (_, _, d0) in ttiles]
            allacc = [("acc", w) for w in range(4)]
            for cc in range(8):
                u = 12 + cc
                load_unit(u + 2)
                W = wv(u, 512)
                wres = ("wr", u % 3)
                for (c0, n, d0) in ttiles:
                    b = next_bank()
                    proj(b, lambda kc: W[:, kc, 0:128], lambda kc, c0=c0, n=n: xT[:, kc, c0:c0 + n], n,
                         reads=[wres] + xres(c0, n))
                    S.add("act", lambda e, b=b, n=n, d0=d0: e.activation(out=hS[:, d0:d0 + n], in_=ps[b][:, 0:n], func=AF.Copy),
                          writes=[("hS", d0)], banks=[b])
                for (c0, n, d0) in ttiles:
                    b = next_bank()
                    proj(b, lambda kc: W[:, kc, 256:384], lambda kc, c0=c0, n=n: xT[:, kc, c0:c0 + n], n,
                         reads=[wres] + xres(c0, n))
                    S.add("dve", lambda e, b=b, n=n, d0=d0: e.tensor_tensor(out=chS[:, d0:d0 + n], in0=ps[b][:, 0:n],
                                                                            in1=hS[:, d0:d0 + n], op=ALU.mult),
                          reads=[("hS", d0)], writes=[("chS", d0)], banks=[b])
                w0 = cst[:, cc * 3 + 0:cc * 3 + 1]
                w1 = cst[:, cc * 3 + 1:cc * 3 + 2]
                w2 = cst[:, cc * 3 + 2:cc * 3 + 3]
                S.add("act", lambda e, w2=w2: e.activation(out=acc[:, :], in_=chS[:, 2:TOK + 2], func=AF.Copy, scale=w2),
                      reads=allch + ["cst"], writes=allacc)
                S.add("dve", lambda e, w1=w1: e.scalar_tensor_tensor(out=acc[:, :], in0=chS[:, 1:TOK + 1], scalar=w1,
                                                                     in1=acc[:, :], op0=ALU.mult, op1=ALU.add),
                      reads=allch + allacc + ["cst"], writes=allacc)
                S.add("dve", lambda e, w0=w0: e.scalar_tensor_tensor(out=acc[:, :], in0=chS[:, 0:TOK], scalar=w0,
                                                                     in1=acc[:, :], op0=ALU.mult, op1=ALU.add),
                      reads=allch + allacc + ["cst"], writes=allacc)
                for w in range(4):
                    b = next_bank()
                    proj(b, lambda kc: W[:, kc, 384:512], lambda kc, w=w: xT[:, kc, TOK + 512 * w:TOK + 512 * (w + 1)], 512,
                         reads=[wres, ("xT", 4 + w)])
                    S.add("act", lambda e, b=b, w=w: e.activation(out=sg[:, 512 * w:512 * (w + 1)], in_=ps[b][:, :], func=AF.Silu),
                          writes=[("sg", w)], banks=[b])
                for w in range(4):
                    b = next_bank()
                    sl = slice(512 * w, 512 * (w + 1))
                    proj(b, lambda kc: W[:, kc, 128:256], lambda kc, w=w: xT[:, kc, TOK + 512 * w:TOK + 512 * (w + 1)], 512,
                         reads=[wres, ("xT", 4 + w)])
                    S.add("dve", lambda e, b=b, sl=sl: e.tensor_tensor(out=acc[:, sl], in0=ps[b][:, :], in1=acc[:, sl], op=ALU.mult),
                          reads=[("acc", w)], writes=[("acc", w)], banks=[b])
                    S.add("dve", lambda e, sl=sl, cc=cc: e.tensor_tensor(out=uT[:, cc, sl], in0=acc[:, sl], in1=sg[:, sl], op=ALU.mult),
                          reads=[("acc", w), ("sg", w)], writes=[("uT", cc, w)])
            if debug:
                S.add("sp", lambda e: [e.dma_start(out=dbg_u[:, :], in_=uT[:].rearrange("p a t -> p (a t)"))],
                      reads=[("uT", cc, w) for cc in range(8) for w in range(4)], dma="dbg")
            S.emit()

        with ExitStack() as p3:
            def sb3(name, shape, dt):
                return p3.enter_context(nc.sbuf_tensor(name, shape, dt))
            gcS = [sb3(f"gcS{i}", [128, 512], F32) for i in range(2)]
            gaS = [sb3(f"gaS{i}", [128, 512], F32) for i in range(2)]
            mS = [sb3(f"mS{i}", [128, 512], F32) for i in range(2)]
            it = 0
            for j in range(8):
                u = 20 + j
                load_unit(u + 2)
                Wg = wr[u % 3][:, 0:2048].rearrange("p (k c) -> p k c", c=256)
                Wco = wr[u % 3][:, 2048:3072].rearrange("p (k c) -> p k c", c=128)
                Wao = wr[u % 3][:, 3072:3584].rearrange("p (k c) -> p k c", c=128)
                wres = ("wr", u % 3)
                for w in range(4):
                    i2 = it % 2
                    it += 1
                    tsl = slice(512 * w, 512 * (w + 1))
                    xsl = slice(TOK + 512 * w, TOK + 512 * (w + 1))
                    b = next_bank()
                    proj(b, lambda kc: Wg[:, kc, 0:128], lambda kc, xsl=xsl: xT[:, kc, xsl], 512, reads=[wres, ("xT", 4 + w)])
                    S.add("act", lambda e, b=b, i2=i2, j=j: e.activation(out=gcS[i2][:, :], in_=ps[b][:, :], func=AF.Sigmoid,
                                                                         bias=cst[:, 24 + j:25 + j]),
                          reads=["cst"], writes=[("gcS", i2)], banks=[b])
                    b = next_bank()
                    proj(b, lambda kc: Wg[:, kc, 128:256], lambda kc, xsl=xsl: xT[:, kc, xsl], 512, reads=[wres, ("xT", 4 + w)])
                    S.add("act", lambda e, b=b, i2=i2, j=j: e.activation(out=gaS[i2][:, :], in_=ps[b][:, :], func=AF.Sigmoid,
                                                                         bias=cst[:, 32 + j:33 + j]),
                          reads=["cst"], writes=[("gaS", i2)], banks=[b])
                    b = next_bank()
                    proj(b, lambda kc: Wco[:, kc, :], lambda kc, tsl=tsl: uT[:, kc, tsl], 512,
                         reads=[wres] + [("uT", cc, w) for cc in range(8)])
                    S.add("dve", lambda e, b=b, i2=i2: e.tensor_tensor(out=mS[i2][:, :], in0=ps[b][:, :], in1=gcS[i2][:, :], op=ALU.mult),
                          reads=[("gcS", i2)], writes=[("mS", i2)], banks=[b])
                    b = next_bank()
                    proj(b, lambda kc: Wao[:, kc, :], lambda kc, tsl=tsl: aT[:, kc, tsl], 512, nk=4,
                         reads=[wres] + [("aT", hp, h, w) for hp in range(4) for h in range(2)])
                    S.add("dve", lambda e, b=b, i2=i2: e.tensor_tensor(out=gaS[i2][:, :], in0=ps[b][:, :], in1=gaS[i2][:, :], op=ALU.mult),
                          reads=[("gaS", i2)], writes=[("gaS", i2)], banks=[b])
                    S.add("dve", lambda e, i2=i2, j=j, tsl=tsl: e.tensor_tensor(out=mT[:, j, tsl], in0=mS[i2][:, :], in1=gaS[i2][:, :], op=ALU.add),
                          reads=[("mS", i2), ("gaS", i2)], writes=[("mT", j, w)])
            if debug:
                S.add("sp", lambda e: [e.dma_start(out=dbg_m[:, :], in_=low[:, :])],
                      reads=[("mT", j, w) for j in range(8) for w in range(4)], dma="dbg")
            S.emit()

        es23.close()
        es123.close()
        with ExitStack() as p4:
            def sb4(name, shape, dt):
                return p4.enter_context(nc.sbuf_tensor(name, shape, dt))
            gb = sb4("gb_sb", [128, 2 * DM], F32)
            NY = 4
            xt = [sb4(f"xt{i}", [128, DM], F32) for i in range(3)]
            ys = [sb4(f"ys{i}", [128, DM], F32) for i in range(NY)]
            ob = [sb4(f"ob{i}", [128, DM], F32) for i in range(3)]
            sa = [sb4(f"sa{i}", [128, 64], F32) for i in range(6)]
            junk = sb4("junk", [128, DM], BF16)
            sd = [sb4(f"sd{i}", [128, 64], F32) for i in range(6)]
            wo = [wv(28 + hf, 512) for hf in range(2)]
            S.add("sp", lambda e: [e.dma_start(out=gb[:], in_=gb_d[:, :])], writes=["gb"], dma="gb")
            S.add("dve", lambda e: e.tensor_copy(out=pp[3][:, :], in_=gb[:, 0:DM]),
                  reads=["gb"], writes=["gbp"], banks=[6, 7])
            out_toks = []

            def load_xt(t):
                S.add("sp", lambda e: [e.dma_start(out=xt[t % 3][:], in_=xtok_d[128 * t:128 * (t + 1), :])],
                      writes=[("xt", t % 3)], dma=f"xt{t % 3}")

            load_xt(0)
            load_xt(1)

            def stage_a1(t):
                i3, iy, i6, pb = t % 3, t % NY, t % 6, t % 3
                if t + 2 < 16:
                    load_xt(t + 2)
                for hf in range(2):
                    proj(2 * pb + hf, lambda kc: mT[:, kc, 128 * t:128 * (t + 1)], lambda kc, hf=hf: wo[hf][:, kc, :], 512,
                         reads=[("mT", j, t // 4) for j in range(8)] + [("wr", (28 + hf) % 3)])
                S.add("dve", lambda e: e.scalar_tensor_tensor(
                    out=ys[iy][:, :], in0=xt[i3][:, :], scalar=ALPHA, in1=pp[pb][:, :], op0=ALU.mult, op1=ALU.add),
                    reads=[("xt", i3)], writes=[("ys", iy)], banks=[2 * pb, 2 * pb + 1])
                S.add("act", lambda e: e.activation(out=junk[:, :], in_=ys[iy][:, :], func=AF.Copy, scale=1.0 / DM,
                                                    accum_out=sa[i6][:, 0:1]),
                      reads=[("ys", iy)], writes=["junk", ("sa", i6, 0)])
                S.add("act", lambda e: e.activation(out=junk[:, :], in_=ys[iy][:, :], func=AF.Square, scale=DM ** -0.5,
                                                    accum_out=sa[i6][:, 1:2]),
                      reads=[("ys", iy)], writes=["junk", ("sa", i6, 1)])

            def stage_a2(t):
                i6 = t % 6
                S.add("dve", lambda e: e.tensor_tensor(out=sd[i6][:, 0:1], in0=sa[i6][:, 0:1], in1=sa[i6][:, 0:1], op=ALU.mult),
                      reads=[("sa", i6, 0)], writes=[("sd", i6, 0)])
                S.add("dve", lambda e: e.tensor_scalar(out=sd[i6][:, 4:5], in0=sa[i6][:, 0:1], scalar1=-1.0, scalar2=None, op0=ALU.mult),
                      reads=[("sa", i6, 0)], writes=[("sd", i6, 4)])
                S.add("dve", lambda e: e.tensor_tensor(out=sd[i6][:, 1:2], in0=sa[i6][:, 1:2], in1=sd[i6][:, 0:1], op=ALU.subtract),
                      reads=[("sa", i6, 1), ("sd", i6, 0)], writes=[("sd", i6, 1)])
                S.add("act", lambda e: e.activation(out=sa[i6][:, 2:3], in_=sd[i6][:, 1:2], func=AF.Sqrt, bias=LN_EPS),
                      reads=[("sd", i6, 1)], writes=[("sa", i6, 2)])

            def stage_b(t):
                i3, iy, i6 = t % 3, t % NY, t % 6
                S.add("dve", lambda e: e.reciprocal(out=sd[i6][:, 2:3], in_=sa[i6][:, 2:3]),
                      reads=[("sa", i6, 2)], writes=[("sd", i6, 2)])
                S.add("dve", lambda e: e.tensor_tensor(out=sd[i6][:, 3:4], in0=sd[i6][:, 4:5], in1=sd[i6][:, 2:3], op=ALU.mult),
                      reads=[("sd", i6, 4), ("sd", i6, 2)], writes=[("sd", i6, 3)])
                S.add("act", lambda e: e.activation(out=ob[i3][:, :], in_=ys[iy][:, :], func=AF.Identity,
                                                    scale=sd[i6][:, 2:3], bias=sd[i6][:, 3:4]),
                      reads=[("ys", iy), ("sd", i6, 2), ("sd", i6, 3)], writes=[("ob", i3)])

            def stage_d(t):
                i3 = t % 3
                S.add("dve", lambda e: e.tensor_tensor(out=ob[i3][:, :], in0=ob[i3][:, :], in1=pp[3][:, :], op=ALU.mult),
                      reads=[("ob", i3), "gbp"], writes=[("ob", i3)], banks=[6, 7])
                S.add("pool", lambda e: e.tensor_tensor(out=ob[i3][:, :], in0=ob[i3][:, :], in1=gb[:, DM:2 * DM], op=ALU.add),
                      reads=[("ob", i3), "gb"], writes=[("ob", i3)])
                tok = S.add("pool", lambda e: [e.dma_start(out=out_d[128 * t:128 * (t + 1), :], in_=ob[i3][:])],
                            reads=[("ob", i3)], dma=f"ob{i3}")
                out_toks.append(tok)

            for t in range(16 + 3):
                if t < 16:
                    stage_a1(t)
                if 0 <= t - 2 < 16:
                    stage_b(t - 2)
                if 0 <= t - 1 < 16:
                    stage_a2(t - 1)
                if 0 <= t - 3 < 16:
                    stage_d(t - 3)
            finals = out_toks[-3:]
            if debug:
                finals = finals + [("d:dbg", S.dma_cnt["d:dbg"], "sp", True)]
            S.emit(final_waits=finals)
    return nc


_NC_CACHE = {}


def _get_nc(debug=False):
    if debug not in _NC_CACHE:
        _NC_CACHE[debug] = build(debug)
    return _NC_CACHE[debug]


def make_in_maps(x, w_in, conv_w, w_conv_out, w_att_out, b_gate, w_o, ln_g, ln_b):
    x2 = np.asarray(x, np.float32).reshape(SEQ, DM)
    ws, cst, gb = _prep_weights(w_in, conv_w, w_conv_out, w_att_out, b_gate, w_o, ln_g, ln_b)
    in_maps = []
    for c in range(NCORES):
        own = x2[c * TOK:(c + 1) * TOK]
        halo = x2[(c - 1) * TOK:c * TOK] if c > 0 else np.zeros((TOK, DM), np.float32)
        xe = np.concatenate([halo, own], axis=0)
        xT = np.ascontiguousarray(xe.T).reshape(8, 128, XC)
        in_maps.append({"xT": xT, "xtok": np.ascontiguousarray(own), "ws": ws,
                        "cst": cst, "gb": gb, "mk": _masks(c)})
    return in_maps


def kernel(x, w_in, conv_w, w_conv_out, w_att_out, b_gate, w_o, ln_g, ln_b):
    nc = _get_nc(False)
    in_maps = make_in_maps(x, w_in, conv_w, w_conv_out, w_att_out, b_gate, w_o, ln_g, ln_b)
    res = run_bass_kernel_spmd(nc, in_maps, core_ids=list(range(NCORES)))
    out = np.concatenate([r["out"] for r in res.results], axis=0)
    return out.reshape(1, SEQ, DM).astype(np.float32)
```
